# Optimizing a Trainium2 kernel written in Bass

```python
import math
import jax, jax.numpy as jnp
from jax import lax
import numpy as np

D_MODEL = 1024
BATCH = 16
SEQ = 256
DEPTH = 2
DEC_BATCH = 2
DEC_SEQ = 4096
PAST_LEN = 512

GRID_W = 64
HEAD_DIM = 64
ROPE_FREQS = HEAD_DIM // 4
ROPE_BASE = 10000.0
EPS = 1e-6
NEG_INF = -1e30
Q_BLOCK = 128
WINDOW = 128
BAND = Q_BLOCK + 2 * WINDOW
A_WIDTH = D_MODEL // 2
B_WIDTH = D_MODEL // 2
POOL_WINDOWS = (2, 4, 8, 16)
N_POOL_GROUPS = len(POOL_WINDOWS)
POOL_GW = A_WIDTH // N_POOL_GROUPS
N_FFT_GROUPS = 4
FFT_GW = B_WIDTH // N_FFT_GROUPS
EVEN_WIDTH = A_WIDTH + B_WIDTH
C_HEADS = D_MODEL // 256
D_HEADS = D_MODEL // 128
D_KV_HEADS = 2
D_GROUP = D_HEADS // D_KV_HEADS
C_Q = C_HEADS * 2 * HEAD_DIM
C_K = C_HEADS * 2 * HEAD_DIM
C_V = C_HEADS * 2 * HEAD_DIM
D_Q = D_HEADS * HEAD_DIM
D_K = D_KV_HEADS * HEAD_DIM
D_V = D_KV_HEADS * HEAD_DIM
ODD_SPLITS = (C_Q, C_Q + C_K, C_Q + C_K + C_V, C_Q + C_K + C_V + D_Q, C_Q + C_K + C_V + D_Q + D_K)
ODD_IN = C_Q + C_K + C_V + D_Q + D_K + D_V
ODD_OUT = C_V + D_Q
D_FF = 4 * D_MODEL
N_EVEN = (DEPTH + 1) // 2
N_ODD = DEPTH // 2
ATTN_SCALE = HEAD_DIM ** -0.5

kernel_name = "hybrid_diffusion_prefix_trunk_step"


def _rms(x, g):
    xf = x.astype(jnp.float32)
    y = xf * lax.rsqrt(jnp.mean(xf * xf, axis=-1, keepdims=True) + EPS)
    return y.astype(x.dtype) * g


def _adaln(cond, w, b):
    m = jax.nn.silu(cond) @ w + b
    return jnp.split(m[:, None, :], 6, axis=-1)


def _channel_mixer(h, w1, w2):
    return jnp.square(jax.nn.relu(h @ w1)) @ w2


def _pool_mix(u, w_pool, scale):
    B, N, _ = u.shape
    ug = u.reshape(B, N, N_POOL_GROUPS, POOL_GW)
    cs = jnp.concatenate([jnp.zeros((B, 1, N_POOL_GROUPS, POOL_GW), jnp.float32),
                          jnp.cumsum(ug.astype(jnp.float32), axis=1)], axis=1)
    t = jnp.arange(N)
    outs = []
    for gi, w in enumerate(POOL_WINDOWS):
        lo = jnp.clip(t - w // 2, 0, N)
        hi = jnp.clip(t + w // 2, 0, N)
        csg = cs[:, :, gi]
        cnt = (hi - lo).astype(jnp.float32)
        outs.append((csg[:, hi] - csg[:, lo]) / cnt[None, :, None])
    pooled = jnp.stack(outs, axis=2).astype(u.dtype) - ug
    y = jnp.einsum('bngc,gcd->bngd', pooled, w_pool)
    return y.reshape(B, N, A_WIDTH) * scale


def _fourier_mix(u, w_fft):
    B, N, _ = u.shape
    uf = u.reshape(B, N, N_FFT_GROUPS, FFT_GW).astype(jnp.float32)
    f = jnp.fft.fft2(uf, axes=(1, 3)).real * ((N * FFT_GW) ** -0.5)
    y = jnp.einsum('bngc,gcd->bngd', f.astype(u.dtype), w_fft)
    return y.reshape(B, N, B_WIDTH)


def _even_mixer(h, w_in, w_pool, pool_scale, w_fft, w_out):
    u = h @ w_in
    a = _pool_mix(u[..., :A_WIDTH], w_pool, pool_scale)
    b = _fourier_mix(u[..., A_WIDTH:], w_fft)
    return jnp.concatenate([a, b], axis=-1) @ w_out


def _axial_rope(n):
    rows = n // GRID_W
    row = jnp.repeat(jnp.arange(rows), GRID_W).astype(jnp.float32)
    col = jnp.tile(jnp.arange(GRID_W), rows).astype(jnp.float32)
    freqs = ROPE_BASE ** (-jnp.arange(ROPE_FREQS, dtype=jnp.float32) / ROPE_FREQS)
    ang = jnp.stack([row[:, None] * freqs, col[:, None] * freqs], axis=1)
    ang = jnp.concatenate([ang, ang], axis=-1).reshape(n, HEAD_DIM)
    return jnp.cos(ang), jnp.sin(ang)


def _apply_rope(x, cos, sin):
    shp = x.shape
    xa = x.reshape(shp[:-1] + (2, 2, ROPE_FREQS))
    rot = jnp.stack([-xa[..., 1, :], xa[..., 0, :]], axis=-2).reshape(shp)
    bshape = (1, shp[1]) + (1,) * (x.ndim - 3) + (HEAD_DIM,)
    return x * cos.reshape(bshape).astype(x.dtype) + rot * sin.reshape(bshape).astype(x.dtype)


def _odd_qkv(h, w_in, c_qn, c_kn, d_qn, d_kn):
    B, N, _ = h.shape
    cq, ck, cv, dq, dk, dv = jnp.split(h @ w_in, ODD_SPLITS, axis=-1)
    cq = _rms(cq.reshape(B, N, C_HEADS, 2, HEAD_DIM), c_qn)
    ck = _rms(ck.reshape(B, N, C_HEADS, 2, HEAD_DIM), c_kn)
    cv = cv.reshape(B, N, C_HEADS, 2 * HEAD_DIM)
    dq = _rms(dq.reshape(B, N, D_KV_HEADS, D_GROUP, HEAD_DIM), d_qn)
    dk = _rms(dk.reshape(B, N, D_KV_HEADS, HEAD_DIM), d_kn)
    dv = dv.reshape(B, N, D_KV_HEADS, HEAD_DIM)
    return cq, ck, cv, dq, dk, dv


def _map_query_blocks(fn, q):
    B, N = q.shape[0], q.shape[1]
    nb = N // Q_BLOCK
    qb = jnp.moveaxis(q.reshape((B, nb, Q_BLOCK) + q.shape[2:]), 1, 0)
    out = lax.map(lambda args: fn(*args), (jnp.arange(nb), qb))
    out = jnp.moveaxis(out, 0, 1)
    return out.reshape((B, N) + out.shape[3:])


def _diff_lambda(lq1, lk1, lq2, lk2, lam_init):
    f = jnp.float32
    return (jnp.exp(jnp.sum(lq1.astype(f) * lk1.astype(f))) -
            jnp.exp(jnp.sum(lq2.astype(f) * lk2.astype(f))) + lam_init)


def _diff_attend_block(qb, k, v, lam):
    s = jnp.einsum('bqhcd,bkhcd->bhcqk', qb, k).astype(jnp.float32) * ATTN_SCALE
    p = jax.nn.softmax(s, axis=-1)
    p = p[:, :, 0] - lam * p[:, :, 1]
    return jnp.einsum('bhqk,bkhe->bqhe', p.astype(v.dtype), v)


def _sink_attend_block(qb, k, v, sink, mask):
    s = jnp.einsum('bqjgd,bkjd->bjgqk', qb, k).astype(jnp.float32) * ATTN_SCALE
    if mask is not None:
        s = jnp.where(mask, s, NEG_INF)
    sk = jnp.broadcast_to(sink.astype(jnp.float32)[None, :, :, None, None], s.shape[:-1] + (1,))
    p = jax.nn.softmax(jnp.concatenate([s, sk], axis=-1), axis=-1)[..., :-1]
    return jnp.einsum('bjgqk,bkjd->bqjgd', p.astype(v.dtype), v)


def _window_attention(q, k, v, ctx_k, ctx_v, sink):
    N = q.shape[1]
    L = ctx_k.shape[1]
    pad = ((0, 0), (WINDOW, WINDOW), (0, 0), (0, 0))
    kp = jnp.pad(k, pad)
    vp = jnp.pad(v, pad)
    qq = jnp.arange(Q_BLOCK)[:, None]
    kk = jnp.arange(BAND)[None, :]
    band_ok = jnp.abs(qq + WINDOW - kk) <= WINDOW
    ctx_ok = jnp.ones((Q_BLOCK, L), bool)

    def fn(b, qb):
        start = b * Q_BLOCK
        kb = lax.dynamic_slice_in_dim(kp, start, BAND, axis=1)
        vb = lax.dynamic_slice_in_dim(vp, start, BAND, axis=1)
        jpos = start - WINDOW + kk
        valid = band_ok & (jpos >= 0) & (jpos < N)
        mask = jnp.concatenate([ctx_ok, valid], axis=1)
        return _sink_attend_block(qb, jnp.concatenate([ctx_k, kb], axis=1),
                                  jnp.concatenate([ctx_v, vb], axis=1), sink, mask)

    return _map_query_blocks(fn, q)


def _odd_output(c_out, d_out, subln_g, lam_init, w_out):
    B, N = c_out.shape[0], c_out.shape[1]
    c = (_rms(c_out, subln_g) * (1.0 - lam_init)).reshape(B, N, C_V)
    d = d_out.reshape(B, N, D_Q)
    return jnp.concatenate([c, d], axis=-1) @ w_out


def _lam_init(layer):
    return 0.8 - 0.6 * math.exp(-0.3 * layer)


def setup_inputs(seed: int = 0) -> dict:
    key = jax.random.key(seed)
    ks = iter(jax.random.split(key, 40))
    f = jnp.float32

    def nrm(shape, s):
        return jax.random.normal(next(ks), shape, f) * s

    def gain(shape):
        return 1.0 + nrm(shape, 0.02)

    return {
        "x_prompt": nrm((BATCH, SEQ, D_MODEL), 1.0),
        "x_sample": nrm((DEC_BATCH, DEC_SEQ, D_MODEL), 1.0),
        "c": nrm((DEC_BATCH, D_MODEL), 1.0),
        "cache_c_k": nrm((DEC_BATCH, N_ODD, PAST_LEN, C_HEADS, 2, HEAD_DIM), 1.0),
        "cache_c_v": nrm((DEC_BATCH, N_ODD, PAST_LEN, C_HEADS, 2 * HEAD_DIM), 1.0),
        "cache_d_k": nrm((DEC_BATCH, N_ODD, PAST_LEN, D_KV_HEADS, HEAD_DIM), 1.0),
        "cache_d_v": nrm((DEC_BATCH, N_ODD, PAST_LEN, D_KV_HEADS, HEAD_DIM), 1.0),
        "c_ctx": nrm((D_MODEL,), 1.0),
        "norm1_g": gain((DEPTH, D_MODEL)),
        "norm2_g": gain((DEPTH, D_MODEL)),
        "w_ada": nrm((DEPTH, D_MODEL, 6 * D_MODEL), 0.5 * D_MODEL ** -0.5),
        "b_ada": nrm((DEPTH, 6 * D_MODEL), 0.01),
        "w_in_even": nrm((N_EVEN, D_MODEL, EVEN_WIDTH), D_MODEL ** -0.5),
        "w_pool": nrm((N_EVEN, N_POOL_GROUPS, POOL_GW, POOL_GW), POOL_GW ** -0.5),
        "pool_scale": gain((N_EVEN, A_WIDTH)),
        "w_fft": nrm((N_EVEN, N_FFT_GROUPS, FFT_GW, FFT_GW), FFT_GW ** -0.5),
        "w_out_even": nrm((N_EVEN, EVEN_WIDTH, D_MODEL), EVEN_WIDTH ** -0.5),
        "w_in_odd": nrm((N_ODD, D_MODEL, ODD_IN), D_MODEL ** -0.5),
        "c_qn_g": gain((N_ODD, HEAD_DIM)),
        "c_kn_g": gain((N_ODD, HEAD_DIM)),
        "lam_q1": nrm((N_ODD, HEAD_DIM), 0.1),
        "lam_k1": nrm((N_ODD, HEAD_DIM), 0.1),
        "lam_q2": nrm((N_ODD, HEAD_DIM), 0.1),
        "lam_k2": nrm((N_ODD, HEAD_DIM), 0.1),
        "c_subln_g": gain((N_ODD, 2 * HEAD_DIM)),
        "d_qn_g": gain((N_ODD, HEAD_DIM)),
        "d_kn_g": gain((N_ODD, HEAD_DIM)),
        "d_sink": nrm((N_ODD, D_HEADS), 0.5),
        "w_out_odd": nrm((N_ODD, ODD_OUT, D_MODEL), ODD_OUT ** -0.5),
        "w_mlp1": nrm((DEPTH, D_MODEL, D_FF), D_MODEL ** -0.5),
        "w_mlp2": nrm((DEPTH, D_FF, D_MODEL), D_FF ** -0.5),
    }


def reference(x_prompt, x_sample, c, cache_c_k, cache_c_v, cache_d_k, cache_d_v, c_ctx,
              norm1_g, norm2_g, w_ada, b_ada, w_in_even, w_pool, pool_scale, w_fft, w_out_even,
              w_in_odd, c_qn_g, c_kn_g, lam_q1, lam_k1, lam_q2, lam_k2, c_subln_g,
              d_qn_g, d_kn_g, d_sink, w_out_odd, w_mlp1, w_mlp2):
    x = x_prompt
    cond_ctx = c_ctx[None, :]
    st_ck, st_cv, st_dk, st_dv = [], [], [], []
    for l in range(DEPTH):
        sh1, sc1, g1, sh2, sc2, g2 = _adaln(cond_ctx, w_ada[l], b_ada[l])
        h = _rms(x, norm1_g[l]) * (1.0 + sc1) + sh1
        if l % 2 == 0:
            e = l // 2
            mix = _even_mixer(h, w_in_even[e], w_pool[e], pool_scale[e], w_fft[e], w_out_even[e])
        else:
            o = l // 2
            cq, ck, cv, dq, dk, dv = _odd_qkv(h, w_in_odd[o], c_qn_g[o], c_kn_g[o], d_qn_g[o], d_kn_g[o])
            lam = _diff_lambda(lam_q1[o], lam_k1[o], lam_q2[o], lam_k2[o], _lam_init(l))
            sink = d_sink[o].reshape(D_KV_HEADS, D_GROUP)
            c_out = _map_query_blocks(lambda b, qb: _diff_attend_block(qb, ck, cv, lam), cq)
            d_out = _map_query_blocks(lambda b, qb: _sink_attend_block(qb, dk, dv, sink, None), dq)
            mix = _odd_output(c_out, d_out, c_subln_g[o], _lam_init(l), w_out_odd[o])
            st_ck.append(ck)
            st_cv.append(cv)
            st_dk.append(dk)
            st_dv.append(dv)
        x = x + g1 * mix
        h = _rms(x, norm2_g[l]) * (1.0 + sc2) + sh2
        x = x + g2 * _channel_mixer(h, w_mlp1[l], w_mlp2[l])
    y_prompt = x
    new_c_k = jnp.stack(st_ck, axis=1)
    new_c_v = jnp.stack(st_cv, axis=1)
    new_d_k = jnp.stack(st_dk, axis=1)
    new_d_v = jnp.stack(st_dv, axis=1)

    x = x_sample
    n_lat = x.shape[1]
    cos, sin = _axial_rope(n_lat)
    for l in range(DEPTH):
        sh1, sc1, g1, sh2, sc2, g2 = _adaln(c, w_ada[l], b_ada[l])
        h = _rms(x, norm1_g[l]) * (1.0 + sc1) + sh1
        if l % 2 == 0:
            e = l // 2
            mix = _even_mixer(h, w_in_even[e], w_pool[e], pool_scale[e], w_fft[e], w_out_even[e])
        else:
            o = l // 2
            cq, ck, cv, dq, dk, dv = _odd_qkv(h, w_in_odd[o], c_qn_g[o], c_kn_g[o], d_qn_g[o], d_kn_g[o])
            cq = _apply_rope(cq, cos, sin)
            ck = _apply_rope(ck, cos, sin)
            dq = _apply_rope(dq, cos, sin)
            dk = _apply_rope(dk, cos, sin)
            lam = _diff_lambda(lam_q1[o], lam_k1[o], lam_q2[o], lam_k2[o], _lam_init(l))
            sink = d_sink[o].reshape(D_KV_HEADS, D_GROUP)
            k_all = jnp.concatenate([cache_c_k[:, o], ck], axis=1)
            v_all = jnp.concatenate([cache_c_v[:, o], cv], axis=1)
            c_out = _map_query_blocks(lambda b, qb: _diff_attend_block(qb, k_all, v_all, lam), cq)
            d_out = _window_attention(dq, dk, dv, cache_d_k[:, o], cache_d_v[:, o], sink)
            mix = _odd_output(c_out, d_out, c_subln_g[o], _lam_init(l), w_out_odd[o])
        x = x + g1 * mix
        h = _rms(x, norm2_g[l]) * (1.0 + sc2) + sh2
        x = x + g2 * _channel_mixer(h, w_mlp1[l], w_mlp2[l])
    y_sample = x
    return (y_prompt, y_sample, new_c_k, new_c_v, new_d_k, new_d_v)
```

```python
import math
import numpy as np
import ml_dtypes
from contextlib import ExitStack
import concourse.bass as bass
import concourse.mybir as mybir
from concourse.bass_utils import run_bass_kernel_spmd

F32 = mybir.dt.float32
BF16 = mybir.dt.bfloat16
AF = mybir.ActivationFunctionType
ALU = mybir.AluOpType
AX = mybir.AxisListType

NCORES = 8
D = 1024
KC = 8
NTOK = 1536
NCOL = 1552
EPS = 1e-6
DEBUG_STAGE = None


class Op:
    __slots__ = ("eng", "fn", "deps", "sig", "sigval", "dkey", "isdma", "inc")


class Sched:
    ENGS = ("pe", "act", "dve", "pool", "sp")

    def __init__(self):
        self.streams = {e: [] for e in self.ENGS}
        self.state = {}
        self.dcount = {}
        self.dall = set()
        self.bufops = {}
        self.alias = {}

    def _note(self, tok, op):
        d = self.bufops.setdefault(tok[0], {"dma": []})
        if op.isdma:
            d["dma"].append(op)
        else:
            d[op.eng] = op

    def retire(self, old_names, new_names):
        ops = []
        for n in old_names:
            d = self.bufops.get(n)
            if not d:
                continue
            for k, v in d.items():
                if k == "dma":
                    ops.extend(v)
                else:
                    ops.append(v)
            ops.extend(self.alias.get(n, []))
        for n in new_names:
            self.alias.setdefault(n, []).extend(ops)
            for tok in [t for t in self.state if t[0] == n]:
                del self.state[tok]
            self.bufops.pop(n, None)

    def add(self, eng, fn, r=(), w=(), dma=False, dkey=None, inc=16, wait_all=False):
        op = Op()
        op.eng, op.fn, op.isdma, op.dkey, op.inc = eng, fn, dma, dkey, inc
        op.sig, op.sigval = False, 0
        deps = {}

        def want(d, raw):
            if d is None or d is op:
                return
            if d.isdma or op.isdma or d.eng != op.eng:
                deps[id(d)] = d
            elif (raw and op.eng != "pe") or op.eng == "pool":
                deps[id(d)] = d

        for tok in list(r) + list(w):
            if tok not in self.state and tok[0] in self.alias:
                for d in self.alias[tok[0]]:
                    want(d, True)
        for tok in r:
            st = self.state.get(tok)
            if st:
                want(st[0], True)
        for tok in w:
            st = self.state.get(tok)
            if st:
                want(st[0], False)
                for d in st[1].values():
                    want(d, False)
                for d in st[2]:
                    want(d, False)
        for tok in r:
            st = self.state.setdefault(tok, [None, {}, []])
            if op.isdma:
                st[2].append(op)
            else:
                st[1][op.eng] = op
            self._note(tok, op)
        for tok in w:
            self.state[tok] = [op, {}, []]
            self._note(tok, op)
        op.deps = list(deps.values())
        for d in op.deps:
            d.sig = True
        if dma:
            assert dkey is not None
            dkey = (eng, dkey)
            op.dkey = dkey
            self.dcount[dkey] = self.dcount.get(dkey, 0) + 1
            op.sigval = self.dcount[dkey] * inc
            if wait_all:
                self.dall.add(dkey)
        self.streams[eng].append(op)
        return op

    def emit(self, nc, es):
        esem = {e: es.enter_context(nc.semaphore("sem_" + e)) for e in self.ENGS}
        dsem = {}
        for i, k in enumerate(self.dcount):
            dsem[k] = es.enter_context(nc.semaphore("dsem%d" % i))
        for e, ops in self.streams.items():
            cnt = 0
            for op in ops:
                if op.isdma:
                    continue
                if op.sig:
                    cnt += 1
                    op.sigval = cnt
        block = es.enter_context(nc.Block())
        streams, dcount, dall = self.streams, self.dcount, self.dall

        def run(name, engine):
            waited = {}
            for op in streams[name]:
                for d in op.deps:
                    if d.isdma:
                        key = ("d", d.dkey)
                        sem = dsem[d.dkey]
                        val = dcount[d.dkey] * d.inc if d.dkey in dall else d.sigval
                    else:
                        key = ("e", d.eng)
                        sem = esem[d.eng]
                        val = d.sigval
                    if waited.get(key, 0) >= val:
                        continue
                    engine.wait_ge(sem, val)
                    waited[key] = val
                ins = op.fn(engine)
                if op.isdma:
                    ins.then_inc(dsem[op.dkey], op.inc)
                elif op.sig:
                    ins.then_inc(esem[name], 1)

        @block.tensor
        def _(e):
            run("pe", e)

        @block.scalar
        def _(e):
            run("act", e)

        @block.vector
        def _(e):
            run("dve", e)

        @block.gpsimd
        def _(e):
            run("pool", e)

        @block.sync
        def _(e):
            run("sp", e)


def T(name, *subs):
    if not subs:
        return [(name, None)]
    return [(name, s) for s in subs]


def _bf(a):
    return np.asarray(a, np.float32).astype(ml_dtypes.bfloat16)


def host_constants(core):
    b, q = core // 4, core % 4
    c = {}
    c["ident_f"] = np.eye(128, dtype=np.float32)
    c["ident_b"] = _bf(np.eye(128))
    k = np.arange(128, dtype=np.float64)
    th = 2 * np.pi * np.outer(k, k) / 128.0
    c["dftc"] = _bf(np.stack([np.cos(th), -np.sin(th)], 0))
    n = np.arange(256, dtype=np.float64)
    ph = 2 * np.pi * (np.outer(n, n) % 256) / 256.0
    sc = (256 * 128) ** -0.5
    t256 = np.stack([np.cos(ph) * sc, np.sin(ph) * sc], 1)
    c["t256"] = _bf(t256.reshape(2, 128, 2, 256).transpose(1, 0, 2, 3))
    n = np.arange(4096, dtype=np.int64)
    kk = np.arange(1024 * q, 1024 * q + 1024, dtype=np.int64)
    ph = 2 * np.pi * ((np.outer(n, kk) % 4096).astype(np.float64)) / 4096.0
    sc = (4096 * 128) ** -0.5
    tc = (np.cos(ph) * sc).reshape(4096, 2, 512)
    ts = (np.sin(ph) * sc).reshape(4096, 2, 512)
    t4096 = np.stack([tc, ts], 0)
    c["t4096"] = _bf(np.ascontiguousarray(t4096.transpose(2, 0, 1, 3)))
    re = np.ones((4, 3, 2, 8), np.float32)
    for gi, w in enumerate((2, 4, 8, 16)):
        for seg in range(3):
            for side in range(2):
                if seg == 2 and ((side == 0 and q != 0) or (side == 1 and q != 3)):
                    continue
                for j in range(8):
                    dist = j if side == 0 else 7 - j
                    if side == 0:
                        cnt = min(w, dist + w // 2)
                    else:
                        cnt = min(w, (dist + 1) + w // 2)
                    re[gi, seg, side, j] = w / cnt
    c["redge"] = np.ascontiguousarray(np.broadcast_to(re.reshape(1, -1), (128, 192))).astype(np.float32)
    hm = np.ones(16, np.float32)
    if q == 0:
        hm[:8] = 0
    if q == 3:
        hm[8:] = 0
    c["halomask"] = np.ascontiguousarray(np.broadcast_to(hm.reshape(1, 16), (128, 16))).astype(np.float32)
    npos = np.arange(1024 * q, 1024 * q + 1024)
    row = (npos // 64).astype(np.float32)
    col = (npos % 64).astype(np.float32)
    freqs = (np.float32(10000.0) ** (-np.arange(16, dtype=np.float32) / np.float32(16))).astype(np.float32)
    ang = np.stack([row[:, None] * freqs[None, :], col[:, None] * freqs[None, :]], 1).astype(np.float32)
    ang = np.concatenate([ang, ang], -1).reshape(1024, 64)
    cosv = np.cos(ang.astype(np.float64)).astype(np.float32)
    sinv = np.sin(ang.astype(np.float64)).astype(np.float32)
    sgn = np.tile(np.concatenate([-np.ones(16), np.ones(16)]), 2).astype(np.float32)
    c["rope"] = np.ascontiguousarray(np.stack([cosv, sinv * sgn[None, :]], 1)).astype(np.float32)
    kk = np.arange(128)[:, None]
    qq = np.arange(128)[None, :]
    bandL = (kk >= qq).astype(np.float32)
    bandR = (kk <= qq).astype(np.float32)
    c["band"] = _bf(np.stack([bandL, bandR], 1))
    mgm = np.zeros((128, 2, 4, 128), np.float32)
    if q - 1 >= 0:
        mgm[:, 0, q - 1, :] = bandL
    if q + 1 <= 3:
        mgm[:, 1, q + 1, :] = bandR
    c["mg"] = _bf(mgm)
    return c


def build_program(debug_stage=None):
    nc = bass.Bass("TRN2", target_bir_lowering=False)
    S = Sched()
    es = ExitStack()

    def din(name, shape, dt=F32):
        return nc.dram_tensor(name, list(shape), dt, kind="ExternalInput").ap()

    def dout(name, shape, dt=F32):
        return nc.dram_tensor(name, list(shape), dt, kind="ExternalOutput").ap()

    def sb(name, shape, dt):
        return es.enter_context(nc.sbuf_tensor(name, list(shape), dt))

    xp_d = din("xp", [512, D])
    xs_d = din("xs", [1024, D])
    xh_d = din("xh", [16, D])
    cond_d = din("cond", [2, D])
    n1g_d = din("norm1_g", [2, D])
    n2g_d = din("norm2_g", [2, D])
    wada_d = din("w_ada", [2, D, 6 * D])
    bada_d = din("b_ada", [2, 6 * D])
    wie_d = din("w_in_even", [D, D])
    wpool_d = din("w_pool", [4, 128, 128])
    pscale_d = din("pool_scale", [512])
    wfft_d = din("w_fft", [4, 128, 128])
    woe_d = din("w_out_even", [D, D])
    wm1_d = din("w_mlp1", [2, D, 4 * D])
    wm2_d = din("w_mlp2", [2, 4 * D, D])
    identf_d = din("ident_f", [128, 128])
    identb_d = din("ident_b", [128, 128], BF16)
    dftc_d = din("dftc", [2, 128, 128], BF16)
    t256_d = din("t256", [128, 2, 2, 256], BF16)
    t4096_d = din("t4096", [2, 2, 4096, 512], BF16)
    redge_d = din("redge", [128, 192])
    halomask_d = din("halomask", [128, 16])

    yp_d = dout("y_prompt", [512, D])
    ys_d = dout("y_sample", [1024, D])

    ag_ub_in = nc.dram_tensor("ag_ub_in", [1024, 512], BF16)
    ag_ub_out = nc.dram_tensor("ag_ub_out", [4096, 512], BF16)
    RG = [[0, 1, 2, 3], [4, 5, 6, 7]]

    xT = sb("xT", [128, KC, NCOL], F32)
    hT = sb("hT", [128, KC, NCOL], BF16)
    identf = sb("identf", [128, 128], F32)
    identb = sb("identb", [128, 128], BF16)
    ones_b = sb("ones_b", [128, 128], BF16)
    vecs = sb("vecs", [128, 128], F32)
    vecs2 = sb("vecs2", [128, 128], F32)
    vT = sb("vT", [128, 256], F32)
    scT = sb("scT", [128, KC, 2], BF16)
    mod = sb("mod", [128, 2, 48, 2], F32)
    Amod = sb("Amod", [128, 2, 2, KC, 2], F32)
    sqb = sb("sqb", [128, 2, 512], BF16)
    rstd = sb("rstd", [128, 2, 512], F32)
    tmpf = sb("tmpf", [128, 2, 512], F32)
    epsc = sb("epsc", [128, 1], F32)
    redge = sb("redge_s", [128, 192], F32)
    halomask = sb("halomask_s", [128, 16], F32)
    wpg = sb("wpg", [128, 6, 4096], BF16)
    SCR = 32320
    scr = sb("scr", [128, SCR], BF16)
    wsm = sb("wsm", [128, 3, 4, 128], BF16)
    wfft_s = sb("wfft_s", [128, 4, 128], BF16)
    dftc = sb("dftc_s", [128, 2, 128], BF16)
    t256 = sb("t256_s", [128, 2, 2, 256], BF16)

    gsm = sb("gsm", [128, 4, 64], F32)
    lamv = sb("lamv", [128, 4, 64], F32)
    gsub = sb("gsub", [128, 2], F32)
    ones1 = sb("ones1", [128, 128], BF16)
    es8 = sb("es8", [128, 8], F32)
    lamw = sb("lamw", [128, 8], F32)

    PS = [es.enter_context(nc.psum_tensor("ps%d" % i, [128, 512], F32)) for i in range(8)]
    psrot = [0]

    def nextps(banks=(0, 1, 2, 3, 4, 5, 6, 7)):
        b = banks[psrot[0] % len(banks)]
        psrot[0] += 1
        return b

    def scr_bf(off, shape):
        n = int(np.prod(shape))
        v = scr[:, off:off + n]
        if len(shape) == 1:
            return v
        names = " ".join("d%d" % i for i in range(len(shape)))
        kw = {"d%d" % i: s for i, s in enumerate(shape[:-1])}
        return v.rearrange("p (%s) -> p %s" % (names, names), **kw)

    def scr_f32(off, shape):
        n = int(np.prod(shape))
        v = scr[:, off:off + 2 * n].bitcast(F32)
        if len(shape) == 1:
            return v
        names = " ".join("d%d" % i for i in range(len(shape)))
        kw = {"d%d" % i: s for i, s in enumerate(shape[:-1])}
        return v.rearrange("p (%s) -> p %s" % (names, names), **kw)

    xstage = scr_f32(19008, [3, D])
    def dma(q, out, in_, r, w, dkey, wait_all=False, **kw):
        S.add(q, lambda e, out=out, in_=in_, kw=kw: e.dma_start(out=out, in_=in_, **kw),
              r=r, w=w, dma=True, dkey=dkey, wait_all=wait_all)

    def mm(out, lhsT, rhs, start, stop, r, w, **kw):
        S.add("pe", lambda e, out=out, lhsT=lhsT, rhs=rhs, start=start, stop=stop, kw=kw:
              e.matmul(out, lhsT, rhs, start=start, stop=stop, **kw), r=r, w=w)

    def tr(out, in_, ident, r, w):
        S.add("pe", lambda e, out=out, in_=in_, ident=ident: e.transpose(out, in_, ident), r=r, w=w)

    def act(out, in_, func, r, w, bias=None, scale=None):
        kw = {}
        if bias is not None:
            kw["bias"] = bias
        if scale is not None:
            kw["scale"] = scale
        S.add("act", lambda e, out=out, in_=in_, func=func, kw=kw: e.activation(out, in_, func, **kw), r=r, w=w)

    def tt(eng, out, in0, in1, op, r, w):
        S.add(eng, lambda e, out=out, in0=in0, in1=in1, op=op: e.tensor_tensor(out, in0, in1, op), r=r, w=w)

    def ts(eng, out, in0, s1, s2, op0, op1, r, w):
        if op1 is None:
            S.add(eng, lambda e, out=out, in0=in0, s1=s1, op0=op0: e.tensor_scalar(out, in0, s1, None, op0), r=r, w=w)
        else:
            S.add(eng, lambda e, out=out, in0=in0, s1=s1, s2=s2, op0=op0, op1=op1:
                  e.tensor_scalar(out, in0, s1, s2, op0, op1), r=r, w=w)

    def stt(out, in0, scalar, in1, op0, op1, r, w):
        S.add("dve", lambda e, out=out, in0=in0, scalar=scalar, in1=in1, op0=op0, op1=op1:
              e.scalar_tensor_tensor(out, in0, scalar, in1, op0, op1), r=r, w=w)

    def cp(eng, out, in_, r, w):
        if eng == "act":
            act(out, in_, AF.Copy, r, w)
        else:
            S.add(eng, lambda e, out=out, in_=in_: e.tensor_copy(out, in_), r=r, w=w)

    def memset(eng, ap, val, w):
        S.add(eng, lambda e, ap=ap, val=val: e.memset(ap, val), r=(), w=w)

    CK = "const"
    dma("sp", identf[:], identf_d, (), T("identf"), CK, wait_all=True)
    dma("sp", identb[:], identb_d, (), T("identb"), CK, wait_all=True)
    dma("sp", dftc[:], dftc_d.rearrange("s p c -> p s c"), (), T("dftc"), CK, wait_all=True)
    dma("sp", t256[:], t256_d, (), T("t256"), CK, wait_all=True)
    dma("sp", redge[:], redge_d, (), T("redge"), CK, wait_all=True)
    dma("sp", halomask[:], halomask_d, (), T("halomask"), CK, wait_all=True)
    dma("sp", vecs[0:96, :], bada_d.rearrange("l (j p) -> (l j) p", p=128), (), T("vecs", 0), CK, wait_all=True)
    dma("sp", vecs[96:112, :], cond_d.rearrange("c (k p) -> (c k) p", p=128), (), T("vecs", 1), CK, wait_all=True)
    dma("sp", vecs[112:128, :], n1g_d.rearrange("l (k p) -> (l k) p", p=128), (), T("vecs", 2), CK, wait_all=True)
    memset("pool", vecs2[:], 0.0, T("vecs2", "z"))
    dma("sp", vecs2[0:16, :], n2g_d.rearrange("l (k p) -> (l k) p", p=128), T("vecs2", "z"), T("vecs2", 0), CK, wait_all=True)
    dma("sp", vecs2[16:20, :], pscale_d.rearrange("(k p) -> k p", p=128), T("vecs2", "z"), T("vecs2", 1), CK, wait_all=True)
    dma("pool", wsm[:, 0, :, :], wpool_d.rearrange("g c d -> c g d"), (), T("wsm", 0), CK, wait_all=True)
    dma("pool", wfft_s[:], wfft_d.rearrange("g c d -> c g d"), (), T("wfft_s"), CK, wait_all=True)
    memset("dve", ones_b[:], 1.0 / 1024.0, T("ones_b"))
    memset("dve", epsc[:], EPS, T("epsc"))

    pb = 7
    tr(PS[pb][:, 0:128], vecs[:], identf[:], T("vecs", 0, 1, 2) + T("identf"), T("ps", pb))
    cp("dve", vT[:, 0:128], PS[pb][:, 0:128], T("ps", pb), T("vT", 0))
    pb = 6
    tr(PS[pb][:, 0:128], vecs2[:], identf[:], T("vecs2", 0, 1, "z") + T("identf"), T("ps", pb))
    cp("dve", vT[:, 128:256], PS[pb][:, 0:128], T("ps", pb), T("vT", 1))
    badaT = vT[:, 0:96].rearrange("p (l j) -> p l j", l=2)
    condT = vT[:, 96:112].rearrange("p (c k) -> p c k", c=2)
    n1gT = vT[:, 112:128].rearrange("p (l k) -> p l k", l=2)
    n2gT = vT[:, 128:144].rearrange("p (l k) -> p l k", l=2)
    pscT = vT[:, 144:148]
    for c in range(2):
        act(scT[:, :, c], condT[:, c, :], AF.Silu, T("vT", 0), T("scT", c))

    for s in range(2):
        pb = nextps()
        for g in range(4):
            mm(PS[pb][:, g * 128:(g + 1) * 128], dftc[:, s, :], wfft_s[:, g, :], True, True,
               T("dftc") + T("wfft_s"), T("ps", pb))
        cp("dve", wsm[:, 1 + s, :, :], PS[pb][:, :].rearrange("p (g d) -> p g d", g=4), T("ps", pb), T("wsm", 1 + s))

    def wview(p0, np_, shape):
        v = wpg[:, p0:p0 + np_, :]
        names = " ".join("d%d" % i for i in range(len(shape)))
        kw = {"d%d" % i: s for i, s in enumerate(shape[:-1])}
        return v.rearrange("p a b -> p (a b)").rearrange("p (%s) -> p %s" % (names, names), **kw)

    def adaln(l, pieces, force_p0=None, phase="both"):
        pb = 4 + l
        for j6 in pieces:
            if phase != "mm":
                i = adaln.cnt
                adaln.cnt += 1
            p0 = 2 * (i % 3) if force_p0 is None else force_p0
            wv = wview(p0, 2, [KC, 1024])
            if phase != "mm":
                dma("pool", wv, wada_d[l, :, j6 * 1024:(j6 + 1) * 1024].rearrange("(k p) c -> p k c", p=128),
                    (), T("wpg", p0, p0 + 1), ("wpg", p0))
            if phase == "dma":
                continue
            for m in range(8):
                col = (j6 * 8 + m) * 2
                for k in range(KC):
                    mm(PS[pb][:, col:col + 2], wv[:, k, m * 128:(m + 1) * 128], scT[:, k, :], k == 0, k == KC - 1,
                       T("wpg", p0, p0 + 1) + T("scT", 0, 1), T("ps", pb))
            for c in range(2):
                tt("dve", mod[:, l, j6 * 8:(j6 + 1) * 8, c],
                   PS[pb][:, j6 * 16:(j6 + 1) * 16].rearrange("p (m c) -> p m c", c=2)[:, :, c],
                   badaT[:, l, j6 * 8:(j6 + 1) * 8], ALU.add, T("ps", pb) + T("vT", 0), T("mod", (l, j6, c)))
            if j6 in (1, 4):
                which = 0 if j6 == 1 else 1
                gT = n1gT if which == 0 else n2gT
                for c in range(2):
                    stt(Amod[:, l, which, :, c], mod[:, l, j6 * 8:(j6 + 1) * 8, c], 1.0, gT[:, l, :], ALU.add, ALU.mult,
                        T("mod", (l, j6, c)) + T("vT", 0, 1), T("Amod", (l, which, c)))
    adaln.cnt = 0

    def Acoef(l, which, k, c):
        return Amod[:, l, which, k, c:c + 1]

    def Bcoef(l, which, k, c):
        j6 = 0 if which == 0 else 3
        return mod[:, l, j6 * 8 + k, c:c + 1]

    def Gcoef(l, which, k, c):
        j6 = 2 if which == 0 else 5
        return mod[:, l, j6 * 8 + k, c:c + 1]

    def modtoks(l, which, c):
        j6b = 0 if which == 0 else 3
        j6g = 2 if which == 0 else 5
        return T("Amod", (l, which, c)) + T("mod", (l, j6b, c), (l, j6g, c))

    adaln(0, [0], force_p0=2, phase="dma")
    adaln(0, [1], force_p0=4, phase="dma")

    def load_x(src, rows, col0, slot, nrows=128):
        dma("sp", xstage[0:nrows, slot, :], src, (), T("xstage", slot), ("xstage", slot))
        for half in range(2):
            pb = nextps((0, 1, 2, 3))
            for j in range(4):
                k = half * 4 + j
                tr(PS[pb][:, j * nrows:(j + 1) * nrows], xstage[0:nrows, slot, k * 128:(k + 1) * 128],
                   identf[0:nrows, 0:nrows], T("xstage", slot) + T("identf"), T("ps", pb))
            cp("act" if half == 0 else "dve", xT[:, half * 4:half * 4 + 4, col0:col0 + nrows],
               PS[pb][:, 0:4 * nrows].rearrange("p (j t) -> p j t", j=4), T("ps", pb),
               [("xT", ("c", col0, half))])

    def xtoks(a, b):
        t = []
        for c0 in range(a - a % 128, b, 128):
            t += [("xT", ("c", c0, 0)), ("xT", ("c", c0, 1))]
        return t


    TILES = [(0, 512, 0), (512, 1024, 1), (1024, 1536, 1), (1536, 1552, 1)]

    def xtile_toks(ti):
        a, b, _ = TILES[ti]
        return xtoks(a, b) + T("xT", ("t", ti))

    nrm_cnt = [0]

    def norm(l, which, ti, mod_eng="pool"):
        a, b, c = TILES[ti]
        n = b - a
        pb = nextps((6, 7))
        i0 = nrm_cnt[0]
        nrm_cnt[0] += 1
        for k in range(KC):
            sl = (i0 * KC + k) % 2
            act(sqb[:, sl, 0:n], xT[:, k, a:b], AF.Square, xtile_toks(ti), T("sqb", sl))
            mm(PS[pb][:, 0:n], ones_b[:], sqb[:, sl, 0:n], k == 0, k == KC - 1, T("sqb", sl) + T("ones_b"), T("ps", pb))
        rs = i0 % 2
        act(rstd[:, rs, 0:n], PS[pb][:, 0:n], AF.Ln, T("ps", pb) + T("epsc"), T("rstd", rs), bias=epsc[:, 0:1])
        act(rstd[:, rs, 0:n], rstd[:, rs, 0:n], AF.Exp, T("rstd", rs), T("rstd", rs), scale=-0.5)
        for k in range(KC):
            sl = (i0 * KC + k) % 2
            tt("dve", tmpf[:, sl, 0:n], xT[:, k, a:b], rstd[:, rs, 0:n], ALU.mult,
               xtile_toks(ti) + T("rstd", rs), T("tmpf", sl))
            if mod_eng == "act":
                act(hT[:, k, a:b], tmpf[:, sl, 0:n], AF.Identity, T("tmpf", sl) + modtoks(l, which, c), T("hT", (ti, k)),
                    bias=Bcoef(l, which, k, c), scale=Acoef(l, which, k, c))
            else:
                ts("pool", hT[:, k, a:b], tmpf[:, sl, 0:n], Acoef(l, which, k, c), Bcoef(l, which, k, c), ALU.mult, ALU.add,
                   T("tmpf", sl) + modtoks(l, which, c), T("hT", (ti, k)))

    def hT_toks(ti):
        return T("hT", *[(ti, k) for k in range(KC)])

    WI = wview(0, 2, [KC, 1024])
    WO = wview(2, 2, [KC, 1024])
    dma("pool", WI, wie_d.rearrange("(k p) c -> p k c", p=128), (), T("wpg", 0, 1), ("wpg", 0))
    UPAD_L = 1584
    o = 0
    Upad = scr_f32(o, [4, UPAD_L]); o += 2 * 4 * UPAD_L
    Sa = scr_f32(o, [UPAD_L]); o += 2 * UPAD_L
    Sb_ = scr_f32(o, [UPAD_L]); o += 2 * UPAD_L
    pooled = scr_bf(o, [4, NTOK]); o += 4 * NTOK
    uBp = scr_bf(o, [4, 512]); o += 4 * 512
    uBst = scr_bf(o, [2, 512]); o += 2 * 512
    Sc = scr_f32(o, [UPAD_L])
    Pp = scr_bf(o, [4, 2, 512]); o += 4 * 2 * 512
    assert o <= SCR, o
    uBall = scr_bf(0, [32, 512])
    assert 32 * 512 <= 2 * 4 * UPAD_L + 4 * UPAD_L + 4 * NTOK

    def ub_chunk(cidx):
        ti = cidx // 4 if cidx < 4 else 1 + (cidx - 4) // 4
        a = cidx * 128
        pb = nextps((0, 1, 2, 3, 4, 5))
        for k in range(KC):
            mm(PS[pb][:, :], hT[:, k, a:a + 128], WI[:, k, 512:1024], k == 0, k == KC - 1,
               T("hT", (ti, k)) + T("wpg", 0, 1), T("ps", pb))
        if cidx < 4:
            cp("act", uBp[:, cidx, :], PS[pb][:, :], T("ps", pb), T("uBp", cidx))
        else:
            sl = cidx % 2
            cp("act", uBst[:, sl, :], PS[pb][:, :], T("ps", pb), T("uBst", sl))
            dma("sp", ag_ub_in.ap()[(cidx - 4) * 128:(cidx - 3) * 128, :], uBst[:, sl, :], T("uBst", sl),
                T("ag_ub_in", cidx), ("uBst", sl))

    slot = 0
    for cidx in range(4):
        load_x(xp_d[cidx * 128:(cidx + 1) * 128, :], 128, cidx * 128, slot)
        slot = (slot + 1) % 3
    adaln(0, [0], force_p0=2, phase="mm")
    adaln(0, [1], force_p0=4, phase="mm")
    adaln(0, [2], force_p0=4, phase="dma")
    dma("pool", WO, woe_d.rearrange("(k p) c -> p k c", p=128), (), T("wpg", 2, 3), ("wpg", 2))
    norm(0, 0, 0, mod_eng="pool")
    for half in range(2):
        for cidx in range(half * 4, half * 4 + 4):
            load_x(xs_d[cidx * 128:(cidx + 1) * 128, :], 128, 512 + cidx * 128, slot)
            slot = (slot + 1) % 3
        norm(0, 0, 1 + half, mod_eng="act" if half == 0 else "pool")
        for cidx in range(4 + half * 4, 8 + half * 4):
            ub_chunk(cidx)
    S.add("pool", lambda e: e.collective_compute("AllGather", ALU.bypass, replica_groups=RG,
                                                 ins=[ag_ub_in.ap().opt()], outs=[ag_ub_out.ap().opt()]),
          r=T("ag_ub_in", *range(4, 12)), w=T("ag_ub_out"), dma=True, dkey="cc_ub", inc=1)
    load_x(xh_d[:, :], 16, 1536, slot, nrows=16)
    norm(0, 0, 3, mod_eng="act")
    memset("pool", Upad[:, :, :], 0.0, T("Upad", "z"))
    S.retire(["xstage"], ["pooled"])
    for cidx in range(4):
        ub_chunk(cidx)

    for m in range(4):
        for ti in range(4):
            a, b, c = TILES[ti]
            n = b - a
            pb = nextps((0, 1, 2, 3, 4, 5))
            for k in range(KC):
                mm(PS[pb][:, 0:n], WI[:, k, m * 128:(m + 1) * 128], hT[:, k, a:b], k == 0, k == KC - 1,
                   T("hT", (ti, k)) + T("wpg", 0, 1), T("ps", pb))
            if ti == 0:
                outv = Upad[:, m, 8:8 + 544].rearrange("p (s t) -> p s t", t=272)[:, :, 0:256]
                cp("act", outv, PS[pb][:, 0:512].rearrange("p (s t) -> p s t", t=256), T("ps", pb) + T("Upad", "z"),
                   T("Upad", (m, 0)))
            elif ti in (1, 2):
                c0 = 552 + (ti - 1) * 512
                cp("act", Upad[:, m, c0:c0 + 512], PS[pb][:, 0:512], T("ps", pb) + T("Upad", "z"), T("Upad", (m, ti)))
            else:
                tt("dve", Upad[:, m, 544:552], PS[pb][:, 0:8], halomask[:, 0:8], ALU.mult,
                   T("ps", pb) + T("halomask") + T("Upad", "z"), T("Upad", (m, 3)))
                tt("dve", Upad[:, m, 1576:1584], PS[pb][:, 8:16], halomask[:, 8:16], ALU.mult,
                   T("ps", pb) + T("halomask") + T("Upad", "z"), T("Upad", (m, 4)))

    L = UPAD_L
    for gi, w in enumerate((2, 4, 8, 16)):
        utoks = T("Upad", *[(gi, s) for s in range(5)])
        U = Upad[:, gi, :]
        bufs = {"Sa": Sa, "Sb": Sb_, "Sc": Sc}
        curname, othname = [("Sa", "Sb"), ("Sb", "Sc"), ("Sa", "Sb"), ("Sb", "Sc")][gi]
        cur, oth = bufs[curname], bufs[othname]
        tt("pool", cur[:, 1:L], U[:, 0:L - 1], U[:, 1:L], ALU.add, utoks, T(curname))
        lo, hi = 1, L
        step = 1
        ww = 2
        while ww < w:
            nlo, nhi = lo + step, hi - step
            tt("pool", oth[:, nlo:nhi], cur[:, nlo - step:nhi - step], cur[:, nlo + step:nhi + step], ALU.add,
               T(curname), T(othname))
            cur, oth, curname, othname = oth, cur, othname, curname
            lo, hi = nlo, nhi
            step *= 2
            ww *= 2
        segs = [(8, 256), (280, 256), (552, 1024)]
        for si, (s0, ln) in enumerate(segs):
            for side in range(2):
                c0 = s0 if side == 0 else s0 + ln - 8
                ro = ((gi * 3 + si) * 2 + side) * 8
                tt("dve", cur[:, c0:c0 + 8], cur[:, c0:c0 + 8], redge[:, ro:ro + 8], ALU.mult,
                   T(curname) + T("redge"), T(curname))
        stt(pooled[:, gi, 0:512].rearrange("p (s t) -> p s t", t=256),
            cur[:, 8:8 + 544].rearrange("p (s t) -> p s t", t=272)[:, :, 0:256], 1.0 / w,
            U[:, 8:8 + 544].rearrange("p (s t) -> p s t", t=272)[:, :, 0:256], ALU.mult, ALU.subtract,
            T(curname) + utoks, T("pooled", (gi, 0)))
        stt(pooled[:, gi, 512:1536], cur[:, 552:1576], 1.0 / w, U[:, 552:1576], ALU.mult, ALU.subtract,
            T(curname) + utoks, T("pooled", (gi, 1)))

    adaln(0, [2], force_p0=4, phase="mm")
    S.retire(["Sc"], ["Pp"])
    for s in range(2):
        for g in range(4):
            for cs in range(2):
                pb = nextps((0, 1, 2, 3, 4, 5))
                for ncn in range(2):
                    mm(PS[pb][:, 0:256], uBp[:, 2 * s + ncn, g * 128:(g + 1) * 128], t256[:, ncn, cs, :], ncn == 0, ncn == 1,
                       T("uBp", 2 * s + ncn) + T("t256"), T("ps", pb))
                cp("act" if cs == 0 else "dve", Pp[:, g, cs, s * 256:(s + 1) * 256], PS[pb][:, 0:256], T("ps", pb),
                   T("Pp", (g, cs, s)))
    for g in range(4):
        pb = nextps((0, 1, 2, 3, 4, 5))
        for cs in range(2):
            mm(PS[pb][:, :], wsm[:, 1 + cs, g, :], Pp[:, g, cs, :], cs == 0, cs == 1,
               T("wsm", 1 + cs) + T("Pp", (g, cs, 0), (g, cs, 1)), T("ps", pb))
        cp("act", hT[:, 4 + g, 0:512], PS[pb][:, :], T("ps", pb), T("hT", (0, 4 + g)))

    adaln(0, [3], force_p0=0, phase="dma")
    for ti in range(3):
        a, b, c = TILES[ti]
        for gi in range(4):
            pb = nextps((0, 1, 2, 3, 4, 5))
            mm(PS[pb][:, :], wsm[:, 0, gi, :], pooled[:, gi, a:b], True, True,
               T("wsm", 0) + T("pooled", (gi, 0 if ti == 0 else 1)), T("ps", pb))
            act(hT[:, gi, a:b], PS[pb][:, :], AF.Identity, T("ps", pb) + T("vT", 1), T("hT", (ti, gi)), scale=pscT[:, gi:gi + 1])

    UB_PARTS = [(0, 6, [0]), (6, 12, [0, 1]), (12, 18, [1, 2]), (18, 32, [2, 3])]
    for pi, (n0, n1, grps) in enumerate(UB_PARTS):
        alias = []
        for g_ in grps:
            alias += T("Upad", *[(g_, s_) for s_ in range(5)])
        if pi == 3:
            alias += T("Sa") + T("Sb")
        dma("sp", uBall[:, n0:n1, :],
            ag_ub_out.ap()[n0 * 128:n1 * 128, :].rearrange("(c p) f -> p c f", p=128),
            T("ag_ub_out"), T("uBall", pi) + alias, ("uBall", pi))

    def ub_part(n_):
        return 0 if n_ < 6 else (1 if n_ < 12 else (2 if n_ < 18 else 3))
    TD = [wview(4, 1, [8, 512]), wview(5, 1, [8, 512])]
    gcnt = 0
    for kt in range(2):
        for cs in range(2):
            for ng in range(4):
                pg = 4 + gcnt % 2
                dma("sp", TD[gcnt % 2], t4096_d[kt, cs, ng * 1024:(ng + 1) * 1024, :].rearrange("(c p) k -> p c k", p=128),
                    (), T("wpg", pg), ("wpg", pg))
                for j in range(8):
                    nchunk = ng * 8 + j
                    for g in range(4):
                        mm(PS[g][:, :], uBall[:, nchunk, g * 128:(g + 1) * 128], TD[gcnt % 2][:, j, :],
                           nchunk == 0, nchunk == 31, T("uBall", ub_part(nchunk)) + T("wpg", pg), T("ps", g))
                gcnt += 1
                if gcnt in (4, 8, 12):
                    pj = 3 + (gcnt // 4 - 1)
                    adaln(0, [pj], force_p0=0, phase="mm")
                    if pj < 5:
                        adaln(0, [pj + 1], force_p0=0, phase="dma")
            for g in range(4):
                cp("act" if g % 2 == 0 else "dve", Pp[:, g, cs, :], PS[g][:, :], T("ps", g), T("Pp", (g, cs, 0), (g, cs, 1)))
        ti = 1 + kt
        a, b, c = TILES[ti]
        for g in range(4):
            pb = nextps((4, 5, 6, 7))
            for cs in range(2):
                mm(PS[pb][:, :], wsm[:, 1 + cs, g, :], Pp[:, g, cs, :], cs == 0, cs == 1,
                   T("wsm", 1 + cs) + T("Pp", (g, cs, 0), (g, cs, 1)), T("ps", pb))
            cp("act", hT[:, 4 + g, a:b], PS[pb][:, :], T("ps", pb), T("hT", (ti, 4 + g)))

    def resid_update(l, which, ti, m, pb):
        a, b, c = TILES[ti]
        stt(xT[:, m, a:b], PS[pb][:, :], Gcoef(l, which, m, c), xT[:, m, a:b], ALU.mult, ALU.add,
            T("ps", pb) + modtoks(l, which, c) + xtile_toks(ti), T("xT", ("t", ti)))

    if debug_stage == "mix":
        for ti in range(3):
            a, b, c = TILES[ti]
            for k in range(KC):
                cp("dve", xT[:, k, a:b], hT[:, k, a:b], T("hT", (ti, k)) + xtile_toks(ti), T("xT", ("t", ti)))
    for ti in range(3):
        if debug_stage == "mix":
            break
        a, b, c = TILES[ti]
        for m in range(8):
            pb = nextps((0, 1, 2, 3, 4, 5))
            for k in range(KC):
                mm(PS[pb][:, :], WO[:, k, m * 128:(m + 1) * 128], hT[:, k, a:b], k == 0, k == KC - 1,
                   T("hT", (ti, k)) + T("wpg", 2, 3), T("ps", pb))
            resid_update(0, 0, ti, m, pb)
        if debug_stage is None and ti >= 1:
            norm(0, 1, ti - 1, mod_eng="pool" if ti % 2 == 1 else "act")
    if debug_stage is None:
        norm(0, 1, 2, mod_eng="act")


    S.retire(["uBall", "uBp", "uBst", "Pp"], ["hid", "relu"])
    hid = scr_bf(0, [2, 4, 512])
    relu_f = scr_f32(2 * 4 * 512, [2, 512])

    def mlp(l, nst=3, per_fg=None, do_norm=True, tile_done=None):
        for ti in range(3):
            if do_norm:
                norm(l, 1, ti, mod_eng="pool" if ti % 2 == 0 else "act")
        NFG = 8
        stage = [0]

        def load_w(fg):
            s = fg % nst
            w1 = wview(2 * s, 1, [KC, 512])
            w2 = wview(2 * s + 1, 1, [4, 1024])
            dma("pool", w1, wm1_d[l, :, fg * 512:(fg + 1) * 512].rearrange("(k p) f -> p k f", p=128),
                (), T("wpg", 2 * s), ("wpg", 2 * s))
            dma("pool", w2, wm2_d[l, fg * 512:(fg + 1) * 512, :].rearrange("(j p) d -> p j d", p=128),
                (), T("wpg", 2 * s + 1), ("wpg", 2 * s + 1))
            return w1, w2

        ws = {}
        ws[0] = load_w(0)
        if nst == 3:
            ws[1] = load_w(1)
        hcnt = [0]

        def mlp1(fg, ti):
            a, b, c = TILES[ti]
            w1, _ = ws[fg]
            hs = hcnt[0] % 2
            hcnt[0] += 1
            for j in range(4):
                pb = nextps((0, 1, 2, 3))
                for k in range(KC):
                    mm(PS[pb][:, :], w1[:, k, j * 128:(j + 1) * 128], hT[:, k, a:b], k == 0, k == KC - 1,
                       T("hT", (ti, k)) + T("wpg", 2 * (fg % nst)), T("ps", pb))
                rs = j % 2
                act(relu_f[:, rs, :], PS[pb][:, :], AF.Relu, T("ps", pb), T("relu", rs))
                stt(hid[:, hs, j, :], relu_f[:, rs, :], 1.0, PS[pb][:, :], ALU.mult, ALU.mult,
                    T("relu", rs) + T("ps", pb), T("hid", (hs, j)))
            return hs

        def mlp2(fg, ti, hs):
            _, w2 = ws[fg]
            a, b, c = TILES[ti]
            for m in range(8):
                pb = nextps((4, 5, 6, 7))
                for j in range(4):
                    mm(PS[pb][:, :], w2[:, j, m * 128:(m + 1) * 128], hid[:, hs, j, :], j == 0, j == 3,
                       T("hid", (hs, j)) + T("wpg", 2 * (fg % nst) + 1), T("ps", pb))
                resid_update(l, 1, ti, m, pb)

        pend = None
        for fg in range(NFG):
            for ti in range(3):
                hs = mlp1(fg, ti)
                if pend is not None:
                    mlp2(*pend)
                    if tile_done is not None and pend[0] == NFG - 1:
                        tile_done(pend[1])
                pend = (fg, ti, hs)
                if nst == 3:
                    if ti == 0 and fg >= 1 and fg + 1 < NFG:
                        ws[fg + 1] = load_w(fg + 1)
                else:
                    if ti == 0 and fg + 1 < NFG:
                        ws[fg + 1] = load_w(fg + 1)
                        if per_fg is not None:
                            per_fg(fg, "dma")
                    if ti == 2 and fg + 1 < NFG and per_fg is not None:
                        per_fg(fg, "mm")
        mlp2(*pend)
        if tile_done is not None:
            tile_done(pend[1])

    ostage = scr_f32(8192, [2, D])

    def store_out(dst, col0, chunks):
        for cidx in chunks:
            sl = cidx % 2
            c0 = col0 + cidx * 128
            ti = c0 // 512
            for half in range(2):
                pb = nextps((0, 1, 2, 3))
                for j in range(4):
                    k = half * 4 + j
                    tr(PS[pb][:, j * 128:(j + 1) * 128], xT[:, k, c0:c0 + 128], identf[:],
                       xtile_toks(ti) + T("identf"), T("ps", pb))
                cp("act" if half == 0 else "dve", ostage[:, sl, half * 512:(half + 1) * 512], PS[pb][:, :],
                   T("ps", pb), T("ostage", (sl, half)))
            dma("sp", dst[cidx * 128:(cidx + 1) * 128, :], ostage[:, sl, :], T("ostage", (sl, 0), (sl, 1)),
                T("outd", (col0, cidx)), ("ostage", sl))

    def store_tile(ti):
        if ti == 0:
            store_out(yp_d, 0, range(4))
        else:
            store_out(ys_d, 512, range((ti - 1) * 4, (ti - 1) * 4 + 4))

    if debug_stage not in ("mix", "xmix"):
        if debug_stage is None:
            mlp(0, nst=2, per_fg=lambda fg, ph: adaln(1, [fg], force_p0=4, phase=ph) if fg < 6 else None, do_norm=False,
                tile_done=lambda ti: norm(1, 0, ti, mod_eng="pool" if ti % 2 == 0 else "act"))
        else:
            mlp(0)
    L1_OUT = []
    if debug_stage is None:
        LAM_INIT = 0.8 - 0.6 * math.exp(-0.3 * 1)
        wio_d = din("w_in_odd", [D, 2304])
        woo_d = din("w_out_odd", [D, D])
        gq_d = [din(nm, [64]) for nm in ("c_qn_g", "c_kn_g", "d_qn_g", "d_kn_g")]
        lam4_d = din("lam4", [4, 64])
        subln_d = din("c_subln_g", [128])
        sink_d = din("d_sink", [8])
        cck_d = din("cache_c_k", [512, 512])
        ccv_d = din("cache_c_v", [512, 512])
        cdk_d = din("cache_d_k", [512, 128])
        cdv_d = din("cache_d_v", [512, 128])
        rope_d = din("rope", [1024, 2, 64])
        band_d = din("band", [128, 2, 128], BF16)
        mg_d = din("mg", [128, 2, 4, 128], BF16)
        nck_d = dout("new_c_k", [512, 512])
        ncv_d = dout("new_c_v", [512, 512])
        ndk_d = dout("new_d_k", [512, 128])
        ndv_d = dout("new_d_v", [512, 128])
        ag_ck_in = nc.dram_tensor("ag_ck_in", [512, 1024], BF16)
        ag_ck_out = nc.dram_tensor("ag_ck_out", [2048, 1024], BF16)
        ag_cv_in = nc.dram_tensor("ag_cv_in", [1024, 512], BF16)
        ag_cv_out = nc.dram_tensor("ag_cv_out", [4096, 512], BF16)
        ag_b_in = nc.dram_tensor("ag_b_in", [256, 384], BF16)
        ag_b_out = nc.dram_tensor("ag_b_out", [1024, 384], BF16)

        S.retire(["t256"], ["mg"])
        S.retire(["dftc"], ["band"])
        S.retire(["vecs"], ["scs"])
        mg = t256[:, :, :, :].rearrange("p a b k -> p (a b k)").rearrange("p (s r t) -> p s r t", s=2, r=4)
        band = dftc
        scs = vecs[:, 0:64]
        wpgf = wpg[:, :, :].rearrange("p a b -> p (a b)")

        def wflat(off, shape):
            n = int(np.prod(shape))
            v = wpgf[:, off:off + n]
            if len(shape) == 1:
                return v
            names = " ".join("d%d" % i for i in range(len(shape)))
            kw = {"d%d" % i: s for i, s in enumerate(shape[:-1])}
            return v.rearrange("p (%s) -> p %s" % (names, names), **kw)

        WQ = wflat(0, [KC, 2304])
        KTb = [wflat(0, [4096]), wflat(8224, [4096])]
        Vb = [wflat(4096, [32, 129]), wflat(12320, [32, 129])]
        WOO = wflat(0, [KC, 1024])
        KTctx = wflat(18920, [4, 512])
        Vctx = wflat(20968, [4, 4, 129])
        dkTctx = wflat(23032, [2, 512])
        Vdctx = wflat(24056, [4, 2, 65])

        qs = scr_f32(0, [2, 2304])
        tmpsq = scr_f32(9216, [1024])
        rt2 = scr_f32(11264, [1024])
        dkdup = scr_f32(13312, [2, 2, 2, 64])
        ckst = scr_bf(14336, [2, 4, 128])
        cvst = scr_bf(15360, [2, 512])
        ssq = scr_f32(16384, [2, 32])
        rsq = scr_f32(16512, [2, 32])
        ropeT = scr_f32(16640, [2, 2, 64])
        cqT_p = scr_bf(18432, [4, 512])
        dqT_p = scr_bf(20480, [4, 512])
        ckT_p = scr_bf(22528, [4, 512])
        Vc_p = scr_bf(24576, [4, 4, 129])
        dkT_p = scr_bf(26640, [2, 512])
        Vd_p = scr_bf(27664, [4, 2, 65])
        cqT_s = scr_bf(18432, [4, 1024])
        dqT_s = scr_bf(22528, [4, 1024])
        dkT_own = scr_bf(26624, [2, 1024])
        Vd_own = scr_bf(28672, [8, 2, 65])
        dkT_g = scr_bf(0, [4, 2, 2, 128])
        Vd_g = scr_bf(2048, [8, 2, 65])

        L0_SCR = ["hid", "relu", "uBall", "uBp", "uBst", "Pp", "pooled", "Upad", "Sa", "Sb", "xstage"]
        QKV_SCR = ["qs", "tmpsq", "rt2", "dkdup", "ckst", "cvst", "ssq", "rsq", "ropeT"]
        P_SCR = ["cqT_p", "dqT_p", "ckT_p", "Vc_p", "dkT_p", "Vd_p"]
        S_SCR = ["cqT_s", "dqT_s", "dkT_own", "Vd_own"]
        G_SCR = ["dkT_g", "Vd_g"]

        S.retire(["wpg"], ["WQ", "KTctx", "Vctx", "dkTctx", "Vdctx"])
        S.retire(L0_SCR, QKV_SCR + P_SCR + ["sqA", "sqB"])
        S.retire(["tmpf"], ["o1", "osq", "cout"])
        S.retire(["rstd"], ["dout"])
        o1 = tmpf[:, 0, :].rearrange("p (q e) -> p q e", q=4)
        osq = tmpf[:, 1, 0:128]
        cout = tmpf[:, 1, 128:384].rearrange("p (s e) -> p s e", s=2)
        dout_s = rstd

        for half in range(2):
            dma("pool", WQ[:, :, half * 1152:(half + 1) * 1152],
                wio_d[:, half * 1152:(half + 1) * 1152].rearrange("(k p) c -> p k c", p=128), (), T("WQ", half), ("WQ", half))
        CK1 = "const1"
        for t in range(4):
            dma("sp", gsm[:, t, :], gq_d[t].partition_broadcast(128), (), T("gsm", t), CK1, wait_all=True)
        dma("sp", lamv[:], lam4_d.partition_broadcast(128), (), T("lamv"), CK1, wait_all=True)
        dma("sp", gsub[:, 0:1], subln_d.rearrange("(p o) -> p o", o=1), (), T("gsub", 0), CK1, wait_all=True)
        memset("dve", ones1[:], 1.0, T("ones1"))
        dma("sp", es8[:], sink_d.partition_broadcast(128), (), T("es8", 0), CK1, wait_all=True)
        dma("sp", band[:], band_d, (), T("band"), CK1, wait_all=True)
        dma("sp", mg[:], mg_d, (), T("mg"), CK1, wait_all=True)
        ts("pool", gsub[:, 1:2], gsub[:, 0:1], 1.0 - LAM_INIT, 0.0, ALU.mult, ALU.add, T("gsub", 0), T("gsub", 1))
        act(es8[:], es8[:], AF.Exp, T("es8", 0), T("es8", 1))
        lprod = osq.rearrange("p (a d) -> p a d", a=2)
        lv = lamv[:].rearrange("p (a b) d -> p a b d", b=2)
        tt("dve", lprod, lv[:, :, 0, :], lv[:, :, 1, :], ALU.mult, T("lamv"), T("osq"))
        S.add("dve", lambda e: e.tensor_reduce(lamw[:, 0:2], lprod, AX.X, ALU.add), r=T("osq"), w=T("lamw", 0))
        act(lamw[:, 2:4], lamw[:, 0:2], AF.Exp, T("lamw", 0), T("lamw", 1))
        tt("dve", lamw[:, 4:5], lamw[:, 3:4], lamw[:, 2:3], ALU.subtract, T("lamw", 1), T("lamw", 2))
        ts("dve", lamw[:, 5:6], lamw[:, 4:5], -LAM_INIT, None, ALU.add, None, T("lamw", 2), T("lamw", 3))
        NEGLAM = lamw[:, 5:6]

        scnt = [0]

        def nexts():
            i = scnt[0] % 64
            scnt[0] += 1
            return i

        dkcnt = [0]

        def dk_post(src, srctoks, dst, dsttoks):
            sl = dkcnt[0] % 2
            dkcnt[0] += 1
            cp("pool", dkdup[:, sl, :, :, :], src.unsqueeze(2).to_broadcast([128, 2, 2, 64]), srctoks, T("dkdup", sl))
            pb = nextps((5, 6, 7))
            for j in range(2):
                tr(PS[pb][:, j * 128:(j + 1) * 128], dkdup[:, sl, j, :, :].rearrange("p u d -> p (u d)"), identf[:],
                   T("dkdup", sl) + T("identf"), T("ps", pb))
            cp("dve", dst, PS[pb][:, 0:256].rearrange("p (j t) -> p j t", j=2), T("ps", pb), dsttoks)

        for kc in range(4):
            sl = kc % 2
            dma("sp", qs[:, sl, 0:512], cck_d[kc * 128:(kc + 1) * 128, :], (), T("qs", (sl, "n1")), ("qs", sl, 1))
            dma("sp", qs[:, sl, 2048:2176], cdk_d[kc * 128:(kc + 1) * 128, :], (), T("qs", (sl, "n2")), ("qs", sl, 2))
            pb = nextps((5, 6, 7))
            for h in range(4):
                tr(PS[pb][:, h * 128:(h + 1) * 128], qs[:, sl, h * 128:(h + 1) * 128], identf[:], T("qs", (sl, "n1")) + T("identf"), T("ps", pb))
            cp("act", KTctx[:, :, kc * 128:(kc + 1) * 128], PS[pb][:, :].rearrange("p (h t) -> p h t", h=4), T("ps", pb), T("KTctx", kc))
            dk_post(qs[:, sl, 2048:2176].rearrange("p (j d) -> p j d", j=2), T("qs", (sl, "n2")),
                    dkTctx[:, :, kc * 128:(kc + 1) * 128], T("dkTctx", kc))
        for kc in range(4):
            dma("pool", Vctx[:, kc, :, 0:128], ccv_d[kc * 128:(kc + 1) * 128, :].rearrange("p (h e) -> p h e", h=4), (),
                T("Vctx", 0), "Vctx")
        memset("dve", Vctx[:, :, :, 128:129], 1.0, T("Vctx", 1))
        for kc in range(4):
            dma("pool", Vdctx[:, kc, :, 0:64], cdv_d[kc * 128:(kc + 1) * 128, :].rearrange("p (j d) -> p j d", j=2), (),
                T("Vdctx", 0), "Vdctx")
        memset("dve", Vdctx[:, :, :, 64:65], 1.0, T("Vdctx", 1))
        KTctx_t = T("KTctx", 0, 1, 2, 3)
        dkTctx_t = T("dkTctx", 0, 1, 2, 3)

        SEGS = [(0, 1024, 16, 0), (1536, 2176, 10, 16)]

        sqA = scr_bf(29712, [1024])
        sqB = scr_bf(30736, [640])
        SQ_SCR = ["sqA", "sqB"]

        def qkv_mm(cidx):
            ti = 0 if cidx < 4 else 1 + (cidx - 4) // 4
            a = cidx * 128
            for nt in range(5):
                wd = 512 if nt < 4 else 256
                for k in range(KC):
                    mm(PS[nt][:, 0:wd], hT[:, k, a:a + 128], WQ[:, k, nt * 512:nt * 512 + wd], k == 0, k == KC - 1,
                       T("hT", (ti, k)) + T("WQ", 0, 1), T("ps", nt))

        def qkv_post(cidx):
            slot = cidx % 2
            QN1, QV, QN2 = T("qs", (slot, "n1")), T("qs", (slot, "v")), T("qs", (slot, "n2"))
            tokmap = [QN1, QN1, QV, QN2, QN2]
            act(sqA[:, 0:512], PS[0][:, 0:512], AF.Square, T("ps", 0), T("sqA", 0))
            act(sqA[:, 512:1024], PS[1][:, 0:512], AF.Square, T("ps", 1), T("sqA", 1))
            act(sqB[:, 0:512], PS[3][:, 0:512], AF.Square, T("ps", 3), T("sqB", 0))
            act(sqB[:, 512:640], PS[4][:, 0:128], AF.Square, T("ps", 4), T("sqB", 1))
            for nt in range(5):
                wd = 512 if nt < 4 else 256
                cp("act", qs[:, slot, nt * 512:nt * 512 + wd], PS[nt][:, 0:wd], T("ps", nt), tokmap[nt])

        def qkv_stats(cidx):
            slot = cidx % 2
            QN1, QV, QN2 = T("qs", (slot, "n1")), T("qs", (slot, "v")), T("qs", (slot, "n2"))
            S.add("dve", lambda e, o=ssq[:, slot, 0:16], i=sqA[:, 0:1024].rearrange("p (g d) -> p g d", d=64):
                  e.tensor_reduce(o, i, AX.X, ALU.add), r=T("sqA", 0, 1), w=T("ssq", (slot, 0)))
            S.add("dve", lambda e, o=ssq[:, slot, 16:26], i=sqB[:, 0:640].rearrange("p (g d) -> p g d", d=64):
                  e.tensor_reduce(o, i, AX.X, ALU.add), r=T("sqB", 0, 1), w=T("ssq", (slot, 16)))
            act(rsq[:, slot, 0:26], ssq[:, slot, 0:26], AF.Ln, T("ssq", (slot, 0), (slot, 16)) + T("epsc"), T("rsq", slot),
                bias=epsc[:, 0:1], scale=1.0 / 64)
            act(rsq[:, slot, 0:26], rsq[:, slot, 0:26], AF.Exp, T("rsq", slot), T("rsq", slot), scale=-0.5)
            v = qs[:, slot, 0:1024].rearrange("p (t g d) -> p t g d", t=2, d=64)
            tt("pool", v, v, gsm[:, 0:2, :].unsqueeze(2).to_broadcast([128, 2, 8, 64]), ALU.mult, QN1 + T("gsm", 0, 1), QN1)
            for (c0, ng, t) in [(1536, 8, 2), (2048, 2, 3)]:
                v = qs[:, slot, c0:c0 + ng * 64].rearrange("p (g d) -> p g d", d=64)
                tt("pool", v, v, gsm[:, t:t + 1, :].to_broadcast([128, ng, 64]), ALU.mult, QN2 + T("gsm", t), QN2)

        def qkv_s2(cidx):
            sample = cidx >= 4
            slot = cidx % 2
            cs = cidx - 4
            QN1, QV, QN2 = T("qs", (slot, "n1")), T("qs", (slot, "v")), T("qs", (slot, "n2"))
            Q = QN1 + QV + QN2
            SEG2 = [(0, 1024, 16, 0, QN1), (1536, 2176, 10, 16, QN2)]
            if sample:
                dma("sp", ropeT[:, slot, :, :], rope_d[cs * 128:(cs + 1) * 128, :, :], (), T("ropeT", slot), ("ropeT", slot))
                for (c0, c1, ng, go, QN) in SEG2:
                    n = c1 - c0
                    v = qs[:, slot, c0:c1].rearrange("p (g d) -> p g d", d=64)
                    t1 = tmpsq[:, 0:n].rearrange("p (g d) -> p g d", d=64)
                    t2 = rt2[:, 0:n].rearrange("p (g d) -> p g d", d=64)
                    tt("dve", t1, v, ropeT[:, slot, 0:1, :].to_broadcast([128, ng, 64]), ALU.mult,
                       QN + T("ropeT", slot), T("tmpsq"))
                    v5 = qs[:, slot, c0:c1].rearrange("p (g a h f) -> p g a h f", a=2, h=2, f=16)
                    t5 = rt2[:, 0:n].rearrange("p (g a h f) -> p g a h f", a=2, h=2, f=16)
                    s4 = ropeT[:, slot, 1, :].rearrange("p (a h f) -> p a h f", a=2, h=2)
                    for hf in range(2):
                        tt("pool", t5[:, :, :, hf, :], v5[:, :, :, 1 - hf, :],
                           s4[:, :, hf, :].unsqueeze(1).to_broadcast([128, ng, 2, 16]), ALU.mult,
                           QN + T("ropeT", slot), T("rt2"))
                    tt("dve", v, t1, t2, ALU.add, T("tmpsq") + T("rt2"), QN)
            for (c0, c1, ng, go, QN) in SEG2:
                v = qs[:, slot, c0:c1].rearrange("p (g d) -> p g d", d=64)
                tt("dve", v, v, rsq[:, slot, go:go + ng].unsqueeze(2).to_broadcast([128, ng, 64]), ALU.mult,
                   QN + T("rsq", slot), QN)
            if not sample:
                r0 = cidx * 128
                dma("sp", nck_d[r0:r0 + 128, :], qs[:, slot, 512:1024], QN1, T("o_nck", cidx), ("qso", slot, 0))
                dma("sp", ncv_d[r0:r0 + 128, :], qs[:, slot, 1024:1536], QV, T("o_ncv", cidx), ("qso", slot, 1))
                dma("sp", ndk_d[r0:r0 + 128, :], qs[:, slot, 2048:2176], QN2, T("o_ndk", cidx), ("qso", slot, 2))
                dma("sp", ndv_d[r0:r0 + 128, :], qs[:, slot, 2176:2304], QN2, T("o_ndv", cidx), ("qso", slot, 3))

            def tr4(c0, QN):
                pb = nextps((5, 6, 7))
                for j in range(4):
                    tr(PS[pb][:, j * 128:(j + 1) * 128], qs[:, slot, c0 + j * 128:c0 + (j + 1) * 128], identf[:],
                       QN + T("identf"), T("ps", pb))
                return pb

            cqT = cqT_s if sample else cqT_p
            dqT = dqT_s if sample else dqT_p
            tc0 = cs * 128 if sample else cidx * 128
            cqn = "cqT_s" if sample else "cqT_p"
            dqn = "dqT_s" if sample else "dqT_p"
            pb = tr4(0, QN1)
            cp("dve", cqT[:, :, tc0:tc0 + 128], PS[pb][:, :].rearrange("p (h t) -> p h t", h=4), T("ps", pb), T(cqn, tc0))
            pb = tr4(1536, QN2)
            cp("act", dqT[:, :, tc0:tc0 + 128], PS[pb][:, :].rearrange("p (h t) -> p h t", h=4), T("ps", pb), T(dqn, tc0))
            pb = tr4(512, QN1)
            if sample:
                cp("dve", ckst[:, slot, :, :], PS[pb][:, :].rearrange("p (h t) -> p h t", h=4), T("ps", pb), T("ckst", slot))
                dma("sp", ag_ck_in.ap().rearrange("(h p) t -> p h t", p=128)[:, :, cs * 128:(cs + 1) * 128], ckst[:, slot, :, :],
                    T("ckst", slot), T("ag_ck_in", cs), ("ckst", slot))
                cp("act", cvst[:, slot, :], qs[:, slot, 1024:1536], QV, T("cvst", slot))
                dma("sp", ag_cv_in.ap()[cs * 128:(cs + 1) * 128, :], cvst[:, slot, :], T("cvst", slot), T("ag_cv_in", cs),
                    ("cvst", slot))
                dk_post(qs[:, slot, 2048:2176].rearrange("p (j d) -> p j d", j=2), QN2,
                        dkT_own[:, :, cs * 128:(cs + 1) * 128], T("dkT_own", cs))
                cp("act", Vd_own[:, cs, :, 0:64], qs[:, slot, 2176:2304].rearrange("p (j d) -> p j d", j=2), QN2, T("Vd_own", cs))
                if cs in (0, 7):
                    side = 0 if cs == 0 else 1
                    dma("sp", ag_b_in.ap()[:, 0:256].rearrange("(j p) (s t) -> p j s t", p=128, s=2)[:, :, side, :],
                        dkT_own[:, :, cs * 128:(cs + 1) * 128], T("dkT_own", cs), T("ag_b_in", (side, 0)), ("agb", side))
                    dma("sp", ag_b_in.ap()[side * 128:(side + 1) * 128, 256:384].rearrange("t (j d) -> t j d", j=2),
                        Vd_own[:, cs, :, 0:64], T("Vd_own", cs), T("ag_b_in", (side, 1)), ("agb", side))
            else:
                cp("dve", ckT_p[:, :, tc0:tc0 + 128], PS[pb][:, :].rearrange("p (h t) -> p h t", h=4), T("ps", pb), T("ckT_p", cidx))
                cp("act", Vc_p[:, cidx, :, 0:128], qs[:, slot, 1024:1536].rearrange("p (h e) -> p h e", h=4), QV, T("Vc_p", cidx))
                dk_post(qs[:, slot, 2048:2176].rearrange("p (j d) -> p j d", j=2), QN2,
                        dkT_p[:, :, tc0:tc0 + 128], T("dkT_p", cidx))
                cp("act", Vd_p[:, cidx, :, 0:64], qs[:, slot, 2176:2304].rearrange("p (j d) -> p j d", j=2), QN2, T("Vd_p", cidx))

        def qkv_run(cids):
            prev = None
            for cidx in cids:
                qkv_mm(cidx)
                qkv_post(cidx)
                if prev is not None:
                    qkv_s2(prev)
                qkv_stats(cidx)
                prev = cidx
            qkv_s2(prev)

        pcnt = [0]
        ccnt = [0]
        dcnt = [0]

        ATT = ["pTr", "gsum", "rden", "o1T", "sqT", "QZ", "tT"]
        pTr = scr_bf(4096, [8, 512])
        gsum = scr_bf(8192, [2, 512])
        rden = scr_f32(9216, [512])
        o1T = scr_f32(10240, [512])
        sqT = scr_bf(11264, [512])
        QZ = scr_bf(12288, [2, 1024])
        tT = scr_f32(14336, [512])
        stc = [0]
        gcnt2 = [0]
        acct = [0]
        dpar = [0]

        def att_begin():
            att_begin_w()
            memset("pool", QZ[64:128, 0, :], 0.0, T("QZ", "z0"))
            memset("pool", QZ[0:64, 1, :], 0.0, T("QZ", "z1"))

        def diff_attend(h, cqT, qtoks, qa, qb_, chunks, hcol0, ti):
            n = qb_ - qa
            nch = len(chunks)
            par = 0
            if n <= 256:
                par = dpar[0] % 2
                dpar[0] += 1
            eo = par * 256
            qo = par * 512
            for c in range(2):
                cp("pool", QZ[c * 64:(c + 1) * 64, c, qo:qo + n], cqT[c * 64:(c + 1) * 64, h, qa:qb_], qtoks + T("QZ", "z%d" % c),
                   T("QZ", (c, par)))
            for c in range(2):
                bO, bD = ((0, 1), (2, 3))[acct[0] % 2]
                acct[0] += 1

                def st(ci, c=c):
                    kT, v, toks = chunks[ci]
                    pb = (4, 5, 6, 7)[stc[0] % 4]
                    stc[0] += 1
                    mm(PS[pb][:, 0:n], kT, QZ[:, c, qo:qo + n], True, True, toks + T("QZ", (c, par), "z%d" % c), T("ps", pb))
                    return pb
                LOOK = 3
                pbs = {}
                for ci in range(min(LOOK, nch)):
                    pbs[ci] = st(ci)
                grp = []
                pend = []
                ngrp = (nch + 3) // 4
                gi = 0
                for ci in range(nch):
                    if ci + LOOK < nch:
                        pbs[ci + LOOK] = st(ci + LOOK)
                    kT, v, toks = chunks[ci]
                    pb = pbs.pop(ci)
                    sl = pcnt[0] % 8
                    pcnt[0] += 1
                    act(pTr[:, sl, 0:n], PS[pb][:, 0:n], AF.Exp, T("ps", pb), T("pTr", sl), scale=0.125)
                    mm(PS[bO][:, 0:n], v[:, 0:128], pTr[:, sl, 0:n], ci == 0, ci == nch - 1, T("pTr", sl) + toks, T("ps", bO))
                    grp.append(sl)
                    if len(grp) == 4 or ci == nch - 1:
                        if len(grp) == 1:
                            src, srct = pTr[:, grp[0], 0:n], T("pTr", grp[0])
                        else:
                            gs = gcnt2[0] % 2
                            gcnt2[0] += 1
                            tt("dve", gsum[:, gs, 0:n], pTr[:, grp[0], 0:n], pTr[:, grp[1], 0:n], ALU.add,
                               T("pTr", grp[0], grp[1]), T("gsum", gs))
                            for s_ in grp[2:]:
                                tt("dve", gsum[:, gs, 0:n], gsum[:, gs, 0:n], pTr[:, s_, 0:n], ALU.add,
                                   T("gsum", gs) + T("pTr", s_), T("gsum", gs))
                            src, srct = gsum[:, gs, 0:n], T("gsum", gs)
                        pend.append((ci + 3, src, srct, gi == 0, gi == ngrp - 1))
                        gi += 1
                        grp = []
                    while pend and (pend[0][0] <= ci or ci == nch - 1):
                        _, src_, srct_, f_, l_ = pend.pop(0)
                        mm(PS[bD][:, 0:n], ones1[:], src_, f_, l_, srct_ + T("ones1"), T("ps", bD))
                    if ci < nch - 1:
                        yield
                def epi(c=c, bO=bO, bD=bD, n=n, eo=eo, par=par, h=h, hcol0=hcol0, ti=ti):
                    act(rden[:, eo:eo + n], PS[bD][:, 0:n], AF.Ln, T("ps", bD), T("rden", par))
                    act(rden[:, eo:eo + n], rden[:, eo:eo + n], AF.Exp, T("rden", par), T("rden", par), scale=-1.0)
                    if c == 0:
                        tt("dve", o1T[:, eo:eo + n], PS[bO][:, 0:n], rden[:, eo:eo + n], ALU.mult, T("ps", bO) + T("rden", par), T("o1T", par))
                    else:
                        stt(tT[:, eo:eo + n], PS[bO][:, 0:n], NEGLAM, rden[:, eo:eo + n], ALU.mult, ALU.mult,
                            T("ps", bO) + T("rden", par) + T("lamw", 3), T("tT", par))
                        tt("dve", o1T[:, eo:eo + n], tT[:, eo:eo + n], o1T[:, eo:eo + n], ALU.add, T("tT", par) + T("o1T", par), T("o1T", par))
                        tt("pool", sqT[:, eo:eo + n], o1T[:, eo:eo + n], o1T[:, eo:eo + n], ALU.mult, T("o1T", par), T("sqT", par))
                        mm(PS[bD][:, 0:n], ones1[:], sqT[:, eo:eo + n], True, True, T("sqT", par) + T("ones1"), T("ps", bD))
                        act(rden[:, eo:eo + n], PS[bD][:, 0:n], AF.Ln, T("ps", bD) + T("epsc"), T("rden", par), bias=epsc[:, 0:1], scale=1.0 / 128)
                        act(rden[:, eo:eo + n], rden[:, eo:eo + n], AF.Exp, T("rden", par), T("rden", par), scale=-0.5)
                        stt(hT[:, h, hcol0:hcol0 + n], o1T[:, eo:eo + n], gsub[:, 1:2], rden[:, eo:eo + n], ALU.mult, ALU.mult,
                            T("o1T", par) + T("rden", par) + T("gsub", 1), T("hT", (ti, h)))
                flush_epi()
                epi_q.append(epi)
                yield

        epi_q = []

        def flush_epi():
            while epi_q:
                epi_q.pop(0)()

        QZw = scr_bf(15360, [2, 2, 2, 128])
        ATT.append("QZw")
        wcnt = [0]
        wfin = [0]
        wstc = [0]
        wpc = [0]

        def att_begin_w():
            memset("pool", QZw[64:128, :, :, 0, :], 0.0, T("QZw", "z0"))
            memset("pool", QZw[0:64, :, :, 1, :], 0.0, T("QZw", "z1"))

        def win_run(dqT, qtoks, blocks):
            units = [(bi_, j) for bi_ in range(len(blocks)) for j in range(2)]

            def prep(u):
                bi_, j = units[u]
                qa = blocks[bi_][0]
                ws = u % 2
                for hf in range(2):
                    cp("dve", QZw[hf * 64:(hf + 1) * 64, ws, :, hf, :], dqT[hf * 64:(hf + 1) * 64, j * 2:j * 2 + 2, qa:qa + 128],
                       qtoks + T("QZw", "z%d" % hf), T("QZw", (ws, hf)))
            prep(0)
            for u, (bi_, j) in enumerate(units):
                qa, chunks_fn, hcol, ti = blocks[bi_]
                if u + 1 < len(units):
                    prep(u + 1)
                ws = u % 2
                accb = (3, 1)[u % 2]
                ds_ = bi_ % 2
                chunks = chunks_fn(j)
                nch = len(chunks)
                qz = QZw[:, ws, :, :, :].rearrange("p a b q -> p (a b q)")
                qzt = T("QZw", (ws, 0), (ws, 1), "z0", "z1")

                def st(ci, chunks=chunks, qz=qz, qzt=qzt):
                    kT, v, toks, mask, mtoks = chunks[ci]
                    pb = (0, 2, 6, 7)[wstc[0] % 4]
                    wstc[0] += 1
                    mm(PS[pb][:, :], kT, qz, True, True, toks + qzt, T("ps", pb))
                    return pb
                pbs = {}
                for ci in range(min(2, nch)):
                    pbs[ci] = st(ci)
                for ci in range(nch):
                    if ci + 2 < nch:
                        pbs[ci + 2] = st(ci + 2)
                    kT, v, toks, mask, mtoks = chunks[ci]
                    pb = pbs.pop(ci)
                    sl = wpc[0] % 2
                    wpc[0] += 1
                    act(sqb[:, sl, :], PS[pb][:, :], AF.Exp, T("ps", pb), T("sqb", sl), scale=0.125)
                    if mask is not None:
                        v3 = sqb[:, sl, :].rearrange("p (g q) -> p g q", g=4)
                        tt("dve", v3, v3, mask.unsqueeze(1).to_broadcast([128, 4, 128]), ALU.mult, T("sqb", sl) + mtoks, T("sqb", sl))
                    for g in range(4):
                        mm(PS[accb][:, g * 65:(g + 1) * 65], sqb[:, sl, g * 128:(g + 1) * 128], v, ci == 0 and g == 0, ci == nch - 1,
                           T("sqb", sl) + toks, T("ps", accb), skip_group_check=True)
                accv = PS[accb][:, 0:260].rearrange("p (g e) -> p g e", e=65)
                wb = (wfin[0] % 8) * 8
                wfin[0] += 1
                tt("dve", scs[:, wb:wb + 4], accv[:, :, 64], es8[:, j * 4:j * 4 + 4], ALU.add,
                   T("ps", accb) + T("es8", 1), T("scs", ("w", wb)))
                S.add("dve", lambda e, o=scs[:, wb + 4:wb + 8], x=scs[:, wb:wb + 4]: e.reciprocal(o, x),
                      r=T("scs", ("w", wb)), w=T("scs", ("w", wb + 4)))
                tt("dve", dout_s[:, ds_, j * 256:(j + 1) * 256].rearrange("p (g d) -> p g d", g=4), accv[:, :, 0:64],
                   scs[:, wb + 4:wb + 8].unsqueeze(2).to_broadcast([128, 4, 64]), ALU.mult,
                   T("ps", accb) + T("scs", ("w", wb + 4)), T("dout", ds_))
                if j == 1:
                    pb = (4, 5)[bi_ % 2]
                    for m in range(4):
                        tr(PS[pb][:, m * 128:(m + 1) * 128], dout_s[:, ds_, m * 128:(m + 1) * 128], identf[:],
                           T("dout", ds_) + T("identf"), T("ps", pb))
                    cp("act", hT[:, 4:8, hcol:hcol + 128], PS[pb][:, :].rearrange("p (m t) -> p m t", m=4), T("ps", pb),
                       T("hT", (ti, 4), (ti, 5), (ti, 6), (ti, 7)))

        def chain(gens):
            for g_ in gens:
                yield from g_

        def interleave(main, side, ratio):
            side_alive = True
            k = 0
            for _ in main:
                k += 1
                if side_alive and k % ratio == 0:
                    try:
                        next(side)
                    except StopIteration:
                        side_alive = False
            if side_alive:
                for _ in side:
                    pass

        memset("dve", Vc_p[:, :, :, 128:129], 1.0, T("Vc_p", "ones"))
        memset("dve", Vd_p[:, :, :, 64:65], 1.0, T("Vd_p", "ones"))
        qkv_run(range(4))
        S.retire(QKV_SCR, ATT)
        att_begin()
        cqTp_t = T("cqT_p", 0, 128, 256, 384)
        dqTp_t = T("dqT_p", 0, 128, 256, 384)
        pd, pw = [], []
        for s in range(2):
            for h in range(4):
                chunks = [(ckT_p[:, h, (2 * s + kc) * 128:(2 * s + kc + 1) * 128], Vc_p[:, 2 * s + kc, h, :],
                           T("ckT_p", 2 * s + kc) + T("Vc_p", 2 * s + kc, "ones")) for kc in range(2)]
                pd.append(diff_attend(h, cqT_p, cqTp_t, s * 256, (s + 1) * 256, chunks, s * 256, 0))
            for qb in range(2):
                qa = s * 256 + qb * 128

                def chf(j, s=s):
                    return [(dkT_p[:, j, (2 * s + kc) * 128:(2 * s + kc + 1) * 128], Vd_p[:, 2 * s + kc, j, :],
                             T("dkT_p", 2 * s + kc) + T("Vd_p", 2 * s + kc, "ones"), None, []) for kc in range(2)]
                pw.append((qa, chf, qa, 0))
        for _ in chain(pd):
            pass
        flush_epi()
        win_run(dqT_p, dqTp_t, pw)

        S.retire(P_SCR, S_SCR)
        S.retire(ATT, QKV_SCR)
        memset("dve", Vd_own[:, :, :, 64:65], 1.0, T("Vd_own", "ones"))
        qkv_run(range(4, 12))
        S.add("pool", lambda e: e.collective_compute("AllGather", ALU.bypass, replica_groups=RG,
                                                     ins=[ag_b_in.ap().opt()], outs=[ag_b_out.ap().opt()]),
              r=T("ag_b_in", (0, 0), (0, 1), (1, 0), (1, 1)), w=T("ag_b_out"), dma=True, dkey="cc_b", inc=1)
        S.add("pool", lambda e: e.collective_compute("AllGather", ALU.bypass, replica_groups=RG,
                                                     ins=[ag_ck_in.ap().opt()], outs=[ag_ck_out.ap().opt()]),
              r=T("ag_ck_in", *range(8)), w=T("ag_ck_out"), dma=True, dkey="cc_ck", inc=1)
        S.add("pool", lambda e: e.collective_compute("AllGather", ALU.bypass, replica_groups=RG,
                                                     ins=[ag_cv_in.ap().opt()], outs=[ag_cv_out.ap().opt()]),
              r=T("ag_cv_in", *range(8)), w=T("ag_cv_out"), dma=True, dkey="cc_cv", inc=1)
        S.retire(["WQ"], ["KT0", "V0", "KT1", "V1"])
        S.retire(QKV_SCR, G_SCR + ATT)
        att_begin()
        for bi in range(2):
            memset("dve", Vb[bi][:, :, 128:129], 1.0, T("V%d" % bi, "ones"))
        for r in range(4):
            dma("sp", dkT_g[:, r, :, :, :],
                ag_b_out.ap()[r * 256:(r + 1) * 256, 0:256].rearrange("(j p) (s t) -> p j s t", p=128, s=2),
                T("ag_b_out"), T("dkT_g", r), ("dkT_g", r))
        for j in range(2):
            dma("sp", Vd_g[:, :, j, 0:64], ag_b_out.ap()[:, 256 + j * 64:256 + (j + 1) * 64].rearrange("(rs t) d -> t rs d", t=128),
                T("ag_b_out"), T("Vd_g", 0), "Vd_g")
        memset("dve", Vd_g[:, :, :, 64:65], 1.0, T("Vd_g", 1))
        cqTs_t = T("cqT_s", *[i * 128 for i in range(8)])
        dqTs_t = T("dqT_s", *[i * 128 for i in range(8)])

        def win_chunks(b):
            def chf(j):
                ch = []
                for kc in range(4):
                    ch.append((dkTctx[:, j, kc * 128:(kc + 1) * 128], Vdctx[:, kc, j, :],
                               T("dkTctx", kc) + T("Vdctx", 0, 1), None, []))
                for nb in (b - 1, b, b + 1):
                    if 0 <= nb <= 7:
                        mask = band[:, 0, :] if nb == b - 1 else (band[:, 1, :] if nb == b + 1 else None)
                        ch.append((dkT_own[:, j, nb * 128:(nb + 1) * 128], Vd_own[:, nb, j, :],
                                   T("dkT_own", nb) + T("Vd_own", nb, "ones"), mask, T("band") if mask is not None else []))
                if b == 0:
                    for r in range(4):
                        ch.append((dkT_g[:, r, j, 1, :], Vd_g[:, r * 2 + 1, j, :], T("dkT_g", r) + T("Vd_g", 0, 1),
                                   mg[:, 0, r, :], T("mg")))
                if b == 7:
                    for r in range(4):
                        ch.append((dkT_g[:, r, j, 0, :], Vd_g[:, r * 2 + 0, j, :], T("dkT_g", r) + T("Vd_g", 0, 1),
                                   mg[:, 1, r, :], T("mg")))
                return ch
            return chf

        win_run(dqT_s, dqTs_t, [(b * 128, win_chunks(b), 512 + b * 128, 1 + b // 4) for b in (1, 2, 3, 4, 5, 6, 0, 7)])

        def load_head(h):
            bi = h % 2
            dma("sp", KTb[bi].rearrange("p (r t) -> p r t", r=4),
                ag_ck_out.ap().rearrange("(r h p) t -> p h r t", r=4, h=4)[:, h, :, :],
                T("ag_ck_out"), T("KT%d" % bi), ("KT", bi))
            dma("sp", Vb[bi][:, :, 0:128], ag_cv_out.ap()[:, h * 128:(h + 1) * 128].rearrange("(n p) e -> p n e", p=128),
                T("ag_cv_out"), T("V%d" % bi, 0), ("V", bi))

        load_head(0)
        load_head(1)

        def diff_all():
            for h in range(4):
                bi = h % 2
                chunks = [(KTctx[:, h, kc * 128:(kc + 1) * 128], Vctx[:, kc, h, :], T("KTctx", kc) + T("Vctx", 0, 1)) for kc in range(4)]
                chunks += [(KTb[bi][:, n_ * 128:(n_ + 1) * 128], Vb[bi][:, n_, :], T("KT%d" % bi) + T("V%d" % bi, 0, "ones"))
                           for n_ in range(32)]
                for qt in range(2):
                    yield from diff_attend(h, cqT_s, cqTs_t, qt * 512, (qt + 1) * 512, chunks, 512 + qt * 512, 1 + qt)
                if h + 2 < 4:
                    load_head(h + 2)
                if h == 2:
                    S.retire(["KT0", "V0"], ["wout"])
                    dma("pool", WOO, woo_d.rearrange("(k p) c -> p k c", p=128), (), T("wout"), "wout")
        for _ in diff_all():
            pass
        flush_epi()

        S.retire(["o1", "osq", "cout"], ["tmpf"])
        S.retire(["dout"], ["rstd"])
        for ti in range(3):
            a, b, c = TILES[ti]
            for m in range(8):
                pb = nextps((0, 1, 2, 3, 4, 5))
                for k in range(KC):
                    mm(PS[pb][:, :], WOO[:, k, m * 128:(m + 1) * 128], hT[:, k, a:b], k == 0, k == KC - 1,
                       T("hT", (ti, k)) + T("wout"), T("ps", pb))
                resid_update(1, 0, ti, m, pb)
            if ti >= 1:
                norm(1, 1, ti - 1, mod_eng="pool" if ti % 2 == 1 else "act")
        norm(1, 1, 2, mod_eng="act")

        S.retire(["KT0", "V0", "KT1", "V1", "wout", "WQ", "KTctx", "Vctx", "dkTctx", "Vdctx"], ["wpg"])
        S.retire(QKV_SCR + P_SCR + S_SCR + G_SCR + L0_SCR + ATT + ["sqA", "sqB"], ["hid", "relu"])
        S.retire(QKV_SCR + P_SCR + S_SCR + G_SCR + L0_SCR + ATT + ["sqA", "sqB"], ["ostage"])
        mlp(1, do_norm=False, tile_done=store_tile)
        L1_OUT = [("o_nck", c) for c in range(4)] + [("o_ncv", c) for c in range(4)] + \
                 [("o_ndk", c) for c in range(4)] + [("o_ndv", c) for c in range(4)]


    if debug_stage is not None:
        S.retire(["pooled", "uBall", "uBp", "uBst", "Pp", "hid", "relu"], ["ostage"])
        for ti in range(3):
            store_tile(ti)
    S.add("sp", lambda e: e.nop(), r=[("outd", (0, c)) for c in range(4)] + [("outd", (512, c)) for c in range(8)] + L1_OUT, w=())

    S.emit(nc, es)
    es.close()
    return nc


_CACHE = {}


def kernel(**inputs):
    inp = {k: np.asarray(v) for k, v in inputs.items()}
    if "nc" not in _CACHE:
        _CACHE["nc"] = build_program(DEBUG_STAGE)
    nc = _CACHE["nc"]
    xpr, xsm = inp["x_prompt"], inp["x_sample"]
    in_maps = []
    for core in range(NCORES):
        b, q = core // 4, core % 4
        m = {}
        m["xp"] = np.ascontiguousarray(xpr[2 * core:2 * core + 2].reshape(512, D))
        m["xs"] = np.ascontiguousarray(xsm[b, 1024 * q:1024 * q + 1024])
        xh = np.zeros((16, D), np.float32)
        if q > 0:
            xh[0:8] = xsm[b, 1024 * q - 8:1024 * q]
        if q < 3:
            xh[8:16] = xsm[b, 1024 * q + 1024:1024 * q + 1032]
        m["xh"] = xh
        m["cond"] = np.ascontiguousarray(np.stack([inp["c_ctx"], inp["c"][b]], 0))
        m["norm1_g"] = inp["norm1_g"]
        m["norm2_g"] = inp["norm2_g"]
        m["w_ada"] = inp["w_ada"]
        m["b_ada"] = inp["b_ada"]
        m["w_in_even"] = inp["w_in_even"][0]
        m["w_pool"] = inp["w_pool"][0]
        m["pool_scale"] = inp["pool_scale"][0]
        m["w_fft"] = inp["w_fft"][0]
        m["w_out_even"] = inp["w_out_even"][0]
        m["w_mlp1"] = inp["w_mlp1"]
        m["w_mlp2"] = inp["w_mlp2"]
        if DEBUG_STAGE is None:
            m["w_in_odd"] = inp["w_in_odd"][0]
            m["w_out_odd"] = inp["w_out_odd"][0]
            for nm in ("c_qn_g", "c_kn_g", "d_qn_g", "d_kn_g", "c_subln_g", "d_sink"):
                m[nm] = np.ascontiguousarray(inp[nm][0])
            m["lam4"] = np.ascontiguousarray(np.stack([inp["lam_q1"][0], inp["lam_k1"][0], inp["lam_q2"][0], inp["lam_k2"][0]], 0))
            m["cache_c_k"] = np.ascontiguousarray(inp["cache_c_k"][b, 0].reshape(512, 512))
            m["cache_c_v"] = np.ascontiguousarray(inp["cache_c_v"][b, 0].reshape(512, 512))
            m["cache_d_k"] = np.ascontiguousarray(inp["cache_d_k"][b, 0].reshape(512, 128))
            m["cache_d_v"] = np.ascontiguousarray(inp["cache_d_v"][b, 0].reshape(512, 128))
        cst = host_constants(core)
        if DEBUG_STAGE is not None:
            for nm in ("rope", "band", "mg"):
                cst.pop(nm)
        m.update(cst)
        in_maps.append(m)
    res = run_bass_kernel_spmd(nc, in_maps, core_ids=list(range(NCORES)))
    R = res.results
    y_prompt = np.concatenate([R[c]["y_prompt"].reshape(2, 256, D) for c in range(NCORES)], 0)
    y_sample = np.stack([np.concatenate([R[4 * b + q]["y_sample"] for q in range(4)], 0) for b in range(2)], 0)
    if DEBUG_STAGE is not None:
        return y_prompt, y_sample
    nck = np.concatenate([R[c]["new_c_k"].reshape(2, 1, 256, 4, 2, 64) for c in range(NCORES)], 0)
    ncv = np.concatenate([R[c]["new_c_v"].reshape(2, 1, 256, 4, 128) for c in range(NCORES)], 0)
    ndk = np.concatenate([R[c]["new_d_k"].reshape(2, 1, 256, 2, 64) for c in range(NCORES)], 0)
    ndv = np.concatenate([R[c]["new_d_v"].reshape(2, 1, 256, 2, 64) for c in range(NCORES)], 0)
    return y_prompt, y_sample, nck, ncv, ndk, ndv
```

```python
import math
import numpy as np
import ml_dtypes
from contextlib import ExitStack
import concourse.bass as bass
import concourse.mybir as mybir
from concourse.bass_utils import run_bass_kernel_spmd

F32 = mybir.dt.float32
BF16 = mybir.dt.bfloat16
AF = mybir.ActivationFunctionType
ALU = mybir.AluOpType
AX = mybir.AxisListType

NCORES = 8
D = 1024
KC = 8
NTOK = 1536
NCOL = 1552
EPS = 1e-6
DEBUG_STAGE = None


class Op:
    __slots__ = ("eng", "fn", "deps", "sig", "sigval", "dkey", "isdma", "inc")


class Sched:
    ENGS = ("pe", "act", "dve", "pool", "sp")

    def __init__(self):
        self.streams = {e: [] for e in self.ENGS}
        self.state = {}
        self.dcount = {}
        self.dall = set()
        self.bufops = {}
        self.alias = {}

    def _note(self, tok, op):
        d = self.bufops.setdefault(tok[0], {"dma": []})
        if op.isdma:
            d["dma"].append(op)
        else:
            d[op.eng] = op

    def retire(self, old_names, new_names):
        ops = []
        for n in old_names:
            d = self.bufops.get(n)
            if not d:
                continue
            for k, v in d.items():
                if k == "dma":
                    ops.extend(v)
                else:
                    ops.append(v)
            ops.extend(self.alias.get(n, []))
        for n in new_names:
            self.alias.setdefault(n, []).extend(ops)
            for tok in [t for t in self.state if t[0] == n]:
                del self.state[tok]
            self.bufops.pop(n, None)

    def add(self, eng, fn, r=(), w=(), dma=False, dkey=None, inc=16, wait_all=False):
        op = Op()
        op.eng, op.fn, op.isdma, op.dkey, op.inc = eng, fn, dma, dkey, inc
        op.sig, op.sigval = False, 0
        deps = {}

        def want(d, raw):
            if d is None or d is op:
                return
            if d.isdma or op.isdma or d.eng != op.eng:
                deps[id(d)] = d
            elif (raw and op.eng != "pe") or op.eng == "pool":
                deps[id(d)] = d

        for tok in list(r) + list(w):
            if tok not in self.state and tok[0] in self.alias:
                for d in self.alias[tok[0]]:
                    want(d, True)
        for tok in r:
            st = self.state.get(tok)
            if st:
                want(st[0], True)
        for tok in w:
            st = self.state.get(tok)
            if st:
                want(st[0], False)
                for d in st[1].values():
                    want(d, False)
                for d in st[2]:
                    want(d, False)
        for tok in r:
            st = self.state.setdefault(tok, [None, {}, []])
            if op.isdma:
                st[2].append(op)
            else:
                st[1][op.eng] = op
            self._note(tok, op)
        for tok in w:
            self.state[tok] = [op, {}, []]
            self._note(tok, op)
        op.deps = list(deps.values())
        for d in op.deps:
            d.sig = True
        if dma:
            assert dkey is not None
            dkey = (eng, dkey)
            op.dkey = dkey
            self.dcount[dkey] = self.dcount.get(dkey, 0) + 1
            op.sigval = self.dcount[dkey] * inc
            if wait_all:
                self.dall.add(dkey)
        self.streams[eng].append(op)
        return op

    def emit(self, nc, es):
        esem = {e: es.enter_context(nc.semaphore("sem_" + e)) for e in self.ENGS}
        dsem = {}
        for i, k in enumerate(self.dcount):
            dsem[k] = es.enter_context(nc.semaphore("dsem%d" % i))
        for e, ops in self.streams.items():
            cnt = 0
            for op in ops:
                if op.isdma:
                    continue
                if op.sig:
                    cnt += 1
                    op.sigval = cnt
        block = es.enter_context(nc.Block())
        streams, dcount, dall = self.streams, self.dcount, self.dall

        def run(name, engine):
            waited = {}
            for op in streams[name]:
                for d in op.deps:
                    if d.isdma:
                        key = ("d", d.dkey)
                        sem = dsem[d.dkey]
                        val = dcount[d.dkey] * d.inc if d.dkey in dall else d.sigval
                    else:
                        key = ("e", d.eng)
                        sem = esem[d.eng]
                        val = d.sigval
                    if waited.get(key, 0) >= val:
                        continue
                    engine.wait_ge(sem, val)
                    waited[key] = val
                ins = op.fn(engine)
                if op.isdma:
                    ins.then_inc(dsem[op.dkey], op.inc)
                elif op.sig:
                    ins.then_inc(esem[name], 1)

        @block.tensor
        def _(e):
            run("pe", e)

        @block.scalar
        def _(e):
            run("act", e)

        @block.vector
        def _(e):
            run("dve", e)

        @block.gpsimd
        def _(e):
            run("pool", e)

        @block.sync
        def _(e):
            run("sp", e)


def T(name, *subs):
    if not subs:
        return [(name, None)]
    return [(name, s) for s in subs]


def _bf(a):
    return np.asarray(a, np.float32).astype(ml_dtypes.bfloat16)


def host_constants(core):
    b, q = core // 4, core % 4
    c = {}
    c["ident_f"] = np.eye(128, dtype=np.float32)
    c["ident_b"] = _bf(np.eye(128))
    k = np.arange(128, dtype=np.float64)
    th = 2 * np.pi * np.outer(k, k) / 128.0
    c["dftc"] = _bf(np.stack([np.cos(th), -np.sin(th)], 0))
    n = np.arange(256, dtype=np.float64)
    ph = 2 * np.pi * (np.outer(n, n) % 256) / 256.0
    sc = (256 * 128) ** -0.5
    t256 = np.stack([np.cos(ph) * sc, np.sin(ph) * sc], 1)
    c["t256"] = _bf(t256.reshape(2, 128, 2, 256).transpose(1, 0, 2, 3))
    n = np.arange(4096, dtype=np.int64)
    kk = np.arange(1024 * q, 1024 * q + 1024, dtype=np.int64)
    ph = 2 * np.pi * ((np.outer(n, kk) % 4096).astype(np.float64)) / 4096.0
    sc = (4096 * 128) ** -0.5
    tc = (np.cos(ph) * sc).reshape(4096, 2, 512)
    ts = (np.sin(ph) * sc).reshape(4096, 2, 512)
    t4096 = np.stack([tc, ts], 0)
    c["t4096"] = _bf(np.ascontiguousarray(t4096.transpose(2, 0, 1, 3)))
    re = np.ones((4, 3, 2, 8), np.float32)
    for gi, w in enumerate((2, 4, 8, 16)):
        for seg in range(3):
            for side in range(2):
                if seg == 2 and ((side == 0 and q != 0) or (side == 1 and q != 3)):
                    continue
                for j in range(8):
                    dist = j if side == 0 else 7 - j
                    if side == 0:
                        cnt = min(w, dist + w // 2)
                    else:
                        cnt = min(w, (dist + 1) + w // 2)
                    re[gi, seg, side, j] = w / cnt
    c["redge"] = np.ascontiguousarray(np.broadcast_to(re.reshape(1, -1), (128, 192))).astype(np.float32)
    hm = np.ones(16, np.float32)
    if q == 0:
        hm[:8] = 0
    if q == 3:
        hm[8:] = 0
    c["halomask"] = np.ascontiguousarray(np.broadcast_to(hm.reshape(1, 16), (128, 16))).astype(np.float32)
    npos = np.arange(1024 * q, 1024 * q + 1024)
    row = (npos // 64).astype(np.float32)
    col = (npos % 64).astype(np.float32)
    freqs = (np.float32(10000.0) ** (-np.arange(16, dtype=np.float32) / np.float32(16))).astype(np.float32)
    ang = np.stack([row[:, None] * freqs[None, :], col[:, None] * freqs[None, :]], 1).astype(np.float32)
    ang = np.concatenate([ang, ang], -1).reshape(1024, 64)
    cosv = np.cos(ang.astype(np.float64)).astype(np.float32)
    sinv = np.sin(ang.astype(np.float64)).astype(np.float32)
    sgn = np.tile(np.concatenate([-np.ones(16), np.ones(16)]), 2).astype(np.float32)
    c["rope"] = np.ascontiguousarray(np.stack([cosv, sinv * sgn[None, :]], 1)).astype(np.float32)
    kk = np.arange(128)[:, None]
    qq = np.arange(128)[None, :]
    bandL = (kk >= qq).astype(np.float32)
    bandR = (kk <= qq).astype(np.float32)
    c["band"] = _bf(np.stack([bandL, bandR], 1))
    mgm = np.zeros((128, 2, 4, 128), np.float32)
    if q - 1 >= 0:
        mgm[:, 0, q - 1, :] = bandL
    if q + 1 <= 3:
        mgm[:, 1, q + 1, :] = bandR
    c["mg"] = _bf(mgm)
    return c


def build_program(debug_stage=None):
    nc = bass.Bass("TRN2", target_bir_lowering=False)
    S = Sched()
    es = ExitStack()

    def din(name, shape, dt=F32):
        return nc.dram_tensor(name, list(shape), dt, kind="ExternalInput").ap()

    def dout(name, shape, dt=F32):
        return nc.dram_tensor(name, list(shape), dt, kind="ExternalOutput").ap()

    def sb(name, shape, dt):
        return es.enter_context(nc.sbuf_tensor(name, list(shape), dt))

    xp_d = din("xp", [512, D])
    xs_d = din("xs", [1024, D])
    xh_d = din("xh", [16, D])
    cond_d = din("cond", [2, D])
    n1g_d = din("norm1_g", [2, D])
    n2g_d = din("norm2_g", [2, D])
    wada_d = din("w_ada", [2, D, 6 * D])
    bada_d = din("b_ada", [2, 6 * D])
    wie_d = din("w_in_even", [D, D])
    wpool_d = din("w_pool", [4, 128, 128])
    pscale_d = din("pool_scale", [512])
    wfft_d = din("w_fft", [4, 128, 128])
    woe_d = din("w_out_even", [D, D])
    wm1_d = din("w_mlp1", [2, D, 4 * D])
    wm2_d = din("w_mlp2", [2, 4 * D, D])
    identf_d = din("ident_f", [128, 128])
    identb_d = din("ident_b", [128, 128], BF16)
    dftc_d = din("dftc", [2, 128, 128], BF16)
    t256_d = din("t256", [128, 2, 2, 256], BF16)
    t4096_d = din("t4096", [2, 2, 4096, 512], BF16)
    redge_d = din("redge", [128, 192])
    halomask_d = din("halomask", [128, 16])

    yp_d = dout("y_prompt", [512, D])
    ys_d = dout("y_sample", [1024, D])

    ag_ub_in = nc.dram_tensor("ag_ub_in", [1024, 512], BF16)
    ag_ub_out = nc.dram_tensor("ag_ub_out", [4096, 512], BF16)
    RG = [[0, 1, 2, 3], [4, 5, 6, 7]]

    xT = sb("xT", [128, KC, NCOL], F32)
    hT = sb("hT", [128, KC, NCOL], BF16)
    identf = sb("identf", [128, 128], F32)
    identb = sb("identb", [128, 128], BF16)
    ones_b = sb("ones_b", [128, 128], BF16)
    vecs = sb("vecs", [128, 128], F32)
    vecs2 = sb("vecs2", [128, 128], F32)
    vT = sb("vT", [128, 256], F32)
    scT = sb("scT", [128, KC, 2], BF16)
    mod = sb("mod", [128, 2, 48, 2], F32)
    Amod = sb("Amod", [128, 2, 2, KC, 2], F32)
    sqb = sb("sqb", [128, 2, 512], BF16)
    rstd = sb("rstd", [128, 2, 512], F32)
    tmpf = sb("tmpf", [128, 2, 512], F32)
    epsc = sb("epsc", [128, 1], F32)
    redge = sb("redge_s", [128, 192], F32)
    halomask = sb("halomask_s", [128, 16], F32)
    wpg = sb("wpg", [128, 6, 4096], BF16)
    SCR = 32320
    scr = sb("scr", [128, SCR], BF16)
    wsm = sb("wsm", [128, 3, 4, 128], BF16)
    wfft_s = sb("wfft_s", [128, 4, 128], BF16)
    dftc = sb("dftc_s", [128, 2, 128], BF16)
    t256 = sb("t256_s", [128, 2, 2, 256], BF16)

    gsm = sb("gsm", [128, 4, 64], F32)
    lamv = sb("lamv", [128, 4, 64], F32)
    gsub = sb("gsub", [128, 2], F32)
    ones1 = sb("ones1", [128, 128], BF16)
    es8 = sb("es8", [128, 8], F32)
    lamw = sb("lamw", [128, 8], F32)

    PS = [es.enter_context(nc.psum_tensor("ps%d" % i, [128, 512], F32)) for i in range(8)]
    psrot = [0]

    def nextps(banks=(0, 1, 2, 3, 4, 5, 6, 7)):
        b = banks[psrot[0] % len(banks)]
        psrot[0] += 1
        return b

    def scr_bf(off, shape):
        n = int(np.prod(shape))
        v = scr[:, off:off + n]
        if len(shape) == 1:
            return v
        names = " ".join("d%d" % i for i in range(len(shape)))
        kw = {"d%d" % i: s for i, s in enumerate(shape[:-1])}
        return v.rearrange("p (%s) -> p %s" % (names, names), **kw)

    def scr_f32(off, shape):
        n = int(np.prod(shape))
        v = scr[:, off:off + 2 * n].bitcast(F32)
        if len(shape) == 1:
            return v
        names = " ".join("d%d" % i for i in range(len(shape)))
        kw = {"d%d" % i: s for i, s in enumerate(shape[:-1])}
        return v.rearrange("p (%s) -> p %s" % (names, names), **kw)

    xstage = scr_f32(19008, [3, D])
    def dma(q, out, in_, r, w, dkey, wait_all=False, **kw):
        S.add(q, lambda e, out=out, in_=in_, kw=kw: e.dma_start(out=out, in_=in_, **kw),
              r=r, w=w, dma=True, dkey=dkey, wait_all=wait_all)

    def mm(out, lhsT, rhs, start, stop, r, w, **kw):
        S.add("pe", lambda e, out=out, lhsT=lhsT, rhs=rhs, start=start, stop=stop, kw=kw:
              e.matmul(out, lhsT, rhs, start=start, stop=stop, **kw), r=r, w=w)

    def tr(out, in_, ident, r, w):
        S.add("pe", lambda e, out=out, in_=in_, ident=ident: e.transpose(out, in_, ident), r=r, w=w)

    def act(out, in_, func, r, w, bias=None, scale=None):
        kw = {}
        if bias is not None:
            kw["bias"] = bias
        if scale is not None:
            kw["scale"] = scale
        S.add("act", lambda e, out=out, in_=in_, func=func, kw=kw: e.activation(out, in_, func, **kw), r=r, w=w)

    def tt(eng, out, in0, in1, op, r, w):
        S.add(eng, lambda e, out=out, in0=in0, in1=in1, op=op: e.tensor_tensor(out, in0, in1, op), r=r, w=w)

    def ts(eng, out, in0, s1, s2, op0, op1, r, w):
        if op1 is None:
            S.add(eng, lambda e, out=out, in0=in0, s1=s1, op0=op0: e.tensor_scalar(out, in0, s1, None, op0), r=r, w=w)
        else:
            S.add(eng, lambda e, out=out, in0=in0, s1=s1, s2=s2, op0=op0, op1=op1:
                  e.tensor_scalar(out, in0, s1, s2, op0, op1), r=r, w=w)

    def stt(out, in0, scalar, in1, op0, op1, r, w):
        S.add("dve", lambda e, out=out, in0=in0, scalar=scalar, in1=in1, op0=op0, op1=op1:
              e.scalar_tensor_tensor(out, in0, scalar, in1, op0, op1), r=r, w=w)

    def cp(eng, out, in_, r, w):
        if eng == "act":
            act(out, in_, AF.Copy, r, w)
        else:
            S.add(eng, lambda e, out=out, in_=in_: e.tensor_copy(out, in_), r=r, w=w)

    def memset(eng, ap, val, w):
        S.add(eng, lambda e, ap=ap, val=val: e.memset(ap, val), r=(), w=w)

    CK = "const"
    dma("sp", identf[:], identf_d, (), T("identf"), CK, wait_all=True)
    dma("sp", identb[:], identb_d, (), T("identb"), CK, wait_all=True)
    dma("sp", dftc[:], dftc_d.rearrange("s p c -> p s c"), (), T("dftc"), CK, wait_all=True)
    dma("sp", t256[:], t256_d, (), T("t256"), CK, wait_all=True)
    dma("sp", redge[:], redge_d, (), T("redge"), CK, wait_all=True)
    dma("sp", halomask[:], halomask_d, (), T("halomask"), CK, wait_all=True)
    dma("sp", vecs[0:96, :], bada_d.rearrange("l (j p) -> (l j) p", p=128), (), T("vecs", 0), CK, wait_all=True)
    dma("sp", vecs[96:112, :], cond_d.rearrange("c (k p) -> (c k) p", p=128), (), T("vecs", 1), CK, wait_all=True)
    dma("sp", vecs[112:128, :], n1g_d.rearrange("l (k p) -> (l k) p", p=128), (), T("vecs", 2), CK, wait_all=True)
    memset("pool", vecs2[:], 0.0, T("vecs2", "z"))
    dma("sp", vecs2[0:16, :], n2g_d.rearrange("l (k p) -> (l k) p", p=128), T("vecs2", "z"), T("vecs2", 0), CK, wait_all=True)
    dma("sp", vecs2[16:20, :], pscale_d.rearrange("(k p) -> k p", p=128), T("vecs2", "z"), T("vecs2", 1), CK, wait_all=True)
    dma("pool", wsm[:, 0, :, :], wpool_d.rearrange("g c d -> c g d"), (), T("wsm", 0), CK, wait_all=True)
    dma("pool", wfft_s[:], wfft_d.rearrange("g c d -> c g d"), (), T("wfft_s"), CK, wait_all=True)
    memset("dve", ones_b[:], 1.0 / 1024.0, T("ones_b"))
    memset("dve", epsc[:], EPS, T("epsc"))

    pb = 7
    tr(PS[pb][:, 0:128], vecs[:], identf[:], T("vecs", 0, 1, 2) + T("identf"), T("ps", pb))
    cp("dve", vT[:, 0:128], PS[pb][:, 0:128], T("ps", pb), T("vT", 0))
    pb = 6
    tr(PS[pb][:, 0:128], vecs2[:], identf[:], T("vecs2", 0, 1, "z") + T("identf"), T("ps", pb))
    cp("dve", vT[:, 128:256], PS[pb][:, 0:128], T("ps", pb), T("vT", 1))
    badaT = vT[:, 0:96].rearrange("p (l j) -> p l j", l=2)
    condT = vT[:, 96:112].rearrange("p (c k) -> p c k", c=2)
    n1gT = vT[:, 112:128].rearrange("p (l k) -> p l k", l=2)
    n2gT = vT[:, 128:144].rearrange("p (l k) -> p l k", l=2)
    pscT = vT[:, 144:148]
    for c in range(2):
        act(scT[:, :, c], condT[:, c, :], AF.Silu, T("vT", 0), T("scT", c))

    for s in range(2):
        pb = nextps()
        for g in range(4):
            mm(PS[pb][:, g * 128:(g + 1) * 128], dftc[:, s, :], wfft_s[:, g, :], True, True,
               T("dftc") + T("wfft_s"), T("ps", pb))
        cp("dve", wsm[:, 1 + s, :, :], PS[pb][:, :].rearrange("p (g d) -> p g d", g=4), T("ps", pb), T("wsm", 1 + s))

    def wview(p0, np_, shape):
        v = wpg[:, p0:p0 + np_, :]
        names = " ".join("d%d" % i for i in range(len(shape)))
        kw = {"d%d" % i: s for i, s in enumerate(shape[:-1])}
        return v.rearrange("p a b -> p (a b)").rearrange("p (%s) -> p %s" % (names, names), **kw)

    def adaln(l, pieces, force_p0=None, phase="both"):
        pb = 4 + l
        for j6 in pieces:
            if phase != "mm":
                i = adaln.cnt
                adaln.cnt += 1
            p0 = 2 * (i % 3) if force_p0 is None else force_p0
            wv = wview(p0, 2, [KC, 1024])
            if phase != "mm":
                dma("pool", wv, wada_d[l, :, j6 * 1024:(j6 + 1) * 1024].rearrange("(k p) c -> p k c", p=128),
                    (), T("wpg", p0, p0 + 1), ("wpg", p0))
            if phase == "dma":
                continue
            for m in range(8):
                col = (j6 * 8 + m) * 2
                for k in range(KC):
                    mm(PS[pb][:, col:col + 2], wv[:, k, m * 128:(m + 1) * 128], scT[:, k, :], k == 0, k == KC - 1,
                       T("wpg", p0, p0 + 1) + T("scT", 0, 1), T("ps", pb))
            for c in range(2):
                tt("dve", mod[:, l, j6 * 8:(j6 + 1) * 8, c],
                   PS[pb][:, j6 * 16:(j6 + 1) * 16].rearrange("p (m c) -> p m c", c=2)[:, :, c],
                   badaT[:, l, j6 * 8:(j6 + 1) * 8], ALU.add, T("ps", pb) + T("vT", 0), T("mod", (l, j6, c)))
            if j6 in (1, 4):
                which = 0 if j6 == 1 else 1
                gT = n1gT if which == 0 else n2gT
                for c in range(2):
                    stt(Amod[:, l, which, :, c], mod[:, l, j6 * 8:(j6 + 1) * 8, c], 1.0, gT[:, l, :], ALU.add, ALU.mult,
                        T("mod", (l, j6, c)) + T("vT", 0, 1), T("Amod", (l, which, c)))
    adaln.cnt = 0

    def Acoef(l, which, k, c):
        return Amod[:, l, which, k, c:c + 1]

    def Bcoef(l, which, k, c):
        j6 = 0 if which == 0 else 3
        return mod[:, l, j6 * 8 + k, c:c + 1]

    def Gcoef(l, which, k, c):
        j6 = 2 if which == 0 else 5
        return mod[:, l, j6 * 8 + k, c:c + 1]

    def modtoks(l, which, c):
        j6b = 0 if which == 0 else 3
        j6g = 2 if which == 0 else 5
        return T("Amod", (l, which, c)) + T("mod", (l, j6b, c), (l, j6g, c))

    adaln(0, [0], force_p0=2, phase="dma")
    adaln(0, [1], force_p0=4, phase="dma")

    def load_x(src, rows, col0, slot, nrows=128):
        dma("sp", xstage[0:nrows, slot, :], src, (), T("xstage", slot), ("xstage", slot))
        for half in range(2):
            pb = nextps((0, 1, 2, 3))
            for j in range(4):
                k = half * 4 + j
                tr(PS[pb][:, j * nrows:(j + 1) * nrows], xstage[0:nrows, slot, k * 128:(k + 1) * 128],
                   identf[0:nrows, 0:nrows], T("xstage", slot) + T("identf"), T("ps", pb))
            cp("act" if half == 0 else "dve", xT[:, half * 4:half * 4 + 4, col0:col0 + nrows],
               PS[pb][:, 0:4 * nrows].rearrange("p (j t) -> p j t", j=4), T("ps", pb),
               [("xT", ("c", col0, half))])

    def xtoks(a, b):
        t = []
        for c0 in range(a - a % 128, b, 128):
            t += [("xT", ("c", c0, 0)), ("xT", ("c", c0, 1))]
        return t


    TILES = [(0, 512, 0), (512, 1024, 1), (1024, 1536, 1), (1536, 1552, 1)]

    def xtile_toks(ti):
        a, b, _ = TILES[ti]
        return xtoks(a, b) + T("xT", ("t", ti))

    nrm_cnt = [0]

    def norm(l, which, ti, mod_eng="pool"):
        a, b, c = TILES[ti]
        n = b - a
        pb = nextps((6, 7))
        i0 = nrm_cnt[0]
        nrm_cnt[0] += 1
        for k in range(KC):
            sl = (i0 * KC + k) % 2
            act(sqb[:, sl, 0:n], xT[:, k, a:b], AF.Square, xtile_toks(ti), T("sqb", sl))
            mm(PS[pb][:, 0:n], ones_b[:], sqb[:, sl, 0:n], k == 0, k == KC - 1, T("sqb", sl) + T("ones_b"), T("ps", pb))
        rs = i0 % 2
        act(rstd[:, rs, 0:n], PS[pb][:, 0:n], AF.Ln, T("ps", pb) + T("epsc"), T("rstd", rs), bias=epsc[:, 0:1])
        act(rstd[:, rs, 0:n], rstd[:, rs, 0:n], AF.Exp, T("rstd", rs), T("rstd", rs), scale=-0.5)
        for k in range(KC):
            sl = (i0 * KC + k) % 2
            tt("dve", tmpf[:, sl, 0:n], xT[:, k, a:b], rstd[:, rs, 0:n], ALU.mult,
               xtile_toks(ti) + T("rstd", rs), T("tmpf", sl))
            if mod_eng == "act":
                act(hT[:, k, a:b], tmpf[:, sl, 0:n], AF.Identity, T("tmpf", sl) + modtoks(l, which, c), T("hT", (ti, k)),
                    bias=Bcoef(l, which, k, c), scale=Acoef(l, which, k, c))
            else:
                ts("pool", hT[:, k, a:b], tmpf[:, sl, 0:n], Acoef(l, which, k, c), Bcoef(l, which, k, c), ALU.mult, ALU.add,
                   T("tmpf", sl) + modtoks(l, which, c), T("hT", (ti, k)))

    def hT_toks(ti):
        return T("hT", *[(ti, k) for k in range(KC)])

    WI = wview(0, 2, [KC, 1024])
    WO = wview(2, 2, [KC, 1024])
    dma("pool", WI, wie_d.rearrange("(k p) c -> p k c", p=128), (), T("wpg", 0, 1), ("wpg", 0))
    UPAD_L = 1584
    o = 0
    Upad = scr_f32(o, [4, UPAD_L]); o += 2 * 4 * UPAD_L
    Sa = scr_f32(o, [UPAD_L]); o += 2 * UPAD_L
    Sb_ = scr_f32(o, [UPAD_L]); o += 2 * UPAD_L
    pooled = scr_bf(o, [4, NTOK]); o += 4 * NTOK
    uBp = scr_bf(o, [4, 512]); o += 4 * 512
    uBst = scr_bf(o, [2, 512]); o += 2 * 512
    Sc = scr_f32(o, [UPAD_L])
    Pp = scr_bf(o, [4, 2, 512]); o += 4 * 2 * 512
    assert o <= SCR, o
    uBall = scr_bf(0, [32, 512])
    assert 32 * 512 <= 2 * 4 * UPAD_L + 4 * UPAD_L + 4 * NTOK

    def ub_chunk(cidx):
        ti = cidx // 4 if cidx < 4 else 1 + (cidx - 4) // 4
        a = cidx * 128
        pb = nextps((0, 1, 2, 3, 4, 5))
        for k in range(KC):
            mm(PS[pb][:, :], hT[:, k, a:a + 128], WI[:, k, 512:1024], k == 0, k == KC - 1,
               T("hT", (ti, k)) + T("wpg", 0, 1), T("ps", pb))
        if cidx < 4:
            cp("act", uBp[:, cidx, :], PS[pb][:, :], T("ps", pb), T("uBp", cidx))
        else:
            sl = cidx % 2
            cp("act", uBst[:, sl, :], PS[pb][:, :], T("ps", pb), T("uBst", sl))
            dma("sp", ag_ub_in.ap()[(cidx - 4) * 128:(cidx - 3) * 128, :], uBst[:, sl, :], T("uBst", sl),
                T("ag_ub_in", cidx), ("uBst", sl))

    slot = 0
    for cidx in range(4):
        load_x(xp_d[cidx * 128:(cidx + 1) * 128, :], 128, cidx * 128, slot)
        slot = (slot + 1) % 3
    adaln(0, [0], force_p0=2, phase="mm")
    adaln(0, [1], force_p0=4, phase="mm")
    adaln(0, [2], force_p0=4, phase="dma")
    dma("pool", WO, woe_d.rearrange("(k p) c -> p k c", p=128), (), T("wpg", 2, 3), ("wpg", 2))
    norm(0, 0, 0, mod_eng="pool")
    for half in range(2):
        for cidx in range(half * 4, half * 4 + 4):
            load_x(xs_d[cidx * 128:(cidx + 1) * 128, :], 128, 512 + cidx * 128, slot)
            slot = (slot + 1) % 3
        norm(0, 0, 1 + half, mod_eng="act" if half == 0 else "pool")
        for cidx in range(4 + half * 4, 8 + half * 4):
            ub_chunk(cidx)
    S.add("pool", lambda e: e.collective_compute("AllGather", ALU.bypass, replica_groups=RG,
                                                 ins=[ag_ub_in.ap().opt()], outs=[ag_ub_out.ap().opt()]),
          r=T("ag_ub_in", *range(4, 12)), w=T("ag_ub_out"), dma=True, dkey="cc_ub", inc=1)
    load_x(xh_d[:, :], 16, 1536, slot, nrows=16)
    norm(0, 0, 3, mod_eng="act")
    memset("pool", Upad[:, :, :], 0.0, T("Upad", "z"))
    S.retire(["xstage"], ["pooled"])
    for cidx in range(4):
        ub_chunk(cidx)

    for m in range(4):
        for ti in range(4):
            a, b, c = TILES[ti]
            n = b - a
            pb = nextps((0, 1, 2, 3, 4, 5))
            for k in range(KC):
                mm(PS[pb][:, 0:n], WI[:, k, m * 128:(m + 1) * 128], hT[:, k, a:b], k == 0, k == KC - 1,
                   T("hT", (ti, k)) + T("wpg", 0, 1), T("ps", pb))
            if ti == 0:
                outv = Upad[:, m, 8:8 + 544].rearrange("p (s t) -> p s t", t=272)[:, :, 0:256]
                cp("act", outv, PS[pb][:, 0:512].rearrange("p (s t) -> p s t", t=256), T("ps", pb) + T("Upad", "z"),
                   T("Upad", (m, 0)))
            elif ti in (1, 2):
                c0 = 552 + (ti - 1) * 512
                cp("act", Upad[:, m, c0:c0 + 512], PS[pb][:, 0:512], T("ps", pb) + T("Upad", "z"), T("Upad", (m, ti)))
            else:
                tt("dve", Upad[:, m, 544:552], PS[pb][:, 0:8], halomask[:, 0:8], ALU.mult,
                   T("ps", pb) + T("halomask") + T("Upad", "z"), T("Upad", (m, 3)))
                tt("dve", Upad[:, m, 1576:1584], PS[pb][:, 8:16], halomask[:, 8:16], ALU.mult,
                   T("ps", pb) + T("halomask") + T("Upad", "z"), T("Upad", (m, 4)))

    L = UPAD_L
    for gi, w in enumerate((2, 4, 8, 16)):
        utoks = T("Upad", *[(gi, s) for s in range(5)])
        U = Upad[:, gi, :]
        bufs = {"Sa": Sa, "Sb": Sb_, "Sc": Sc}
        curname, othname = [("Sa", "Sb"), ("Sb", "Sc"), ("Sa", "Sb"), ("Sb", "Sc")][gi]
        cur, oth = bufs[curname], bufs[othname]
        tt("pool", cur[:, 1:L], U[:, 0:L - 1], U[:, 1:L], ALU.add, utoks, T(curname))
        lo, hi = 1, L
        step = 1
        ww = 2
        while ww < w:
            nlo, nhi = lo + step, hi - step
            tt("pool", oth[:, nlo:nhi], cur[:, nlo - step:nhi - step], cur[:, nlo + step:nhi + step], ALU.add,
               T(curname), T(othname))
            cur, oth, curname, othname = oth, cur, othname, curname
            lo, hi = nlo, nhi
            step *= 2
            ww *= 2
        segs = [(8, 256), (280, 256), (552, 1024)]
        for si, (s0, ln) in enumerate(segs):
            for side in range(2):
                c0 = s0 if side == 0 else s0 + ln - 8
                ro = ((gi * 3 + si) * 2 + side) * 8
                tt("dve", cur[:, c0:c0 + 8], cur[:, c0:c0 + 8], redge[:, ro:ro + 8], ALU.mult,
                   T(curname) + T("redge"), T(curname))
        stt(pooled[:, gi, 0:512].rearrange("p (s t) -> p s t", t=256),
            cur[:, 8:8 + 544].rearrange("p (s t) -> p s t", t=272)[:, :, 0:256], 1.0 / w,
            U[:, 8:8 + 544].rearrange("p (s t) -> p s t", t=272)[:, :, 0:256], ALU.mult, ALU.subtract,
            T(curname) + utoks, T("pooled", (gi, 0)))
        stt(pooled[:, gi, 512:1536], cur[:, 552:1576], 1.0 / w, U[:, 552:1576], ALU.mult, ALU.subtract,
            T(curname) + utoks, T("pooled", (gi, 1)))

    adaln(0, [2], force_p0=4, phase="mm")
    S.retire(["Sc"], ["Pp"])
    for s in range(2):
        for g in range(4):
            for cs in range(2):
                pb = nextps((0, 1, 2, 3, 4, 5))
                for ncn in range(2):
                    mm(PS[pb][:, 0:256], uBp[:, 2 * s + ncn, g * 128:(g + 1) * 128], t256[:, ncn, cs, :], ncn == 0, ncn == 1,
                       T("uBp", 2 * s + ncn) + T("t256"), T("ps", pb))
                cp("act" if cs == 0 else "dve", Pp[:, g, cs, s * 256:(s + 1) * 256], PS[pb][:, 0:256], T("ps", pb),
                   T("Pp", (g, cs, s)))
    for g in range(4):
        pb = nextps((0, 1, 2, 3, 4, 5))
        for cs in range(2):
            mm(PS[pb][:, :], wsm[:, 1 + cs, g, :], Pp[:, g, cs, :], cs == 0, cs == 1,
               T("wsm", 1 + cs) + T("Pp", (g, cs, 0), (g, cs, 1)), T("ps", pb))
        cp("act", hT[:, 4 + g, 0:512], PS[pb][:, :], T("ps", pb), T("hT", (0, 4 + g)))

    adaln(0, [3], force_p0=0, phase="dma")
    for ti in range(3):
        a, b, c = TILES[ti]
        for gi in range(4):
            pb = nextps((0, 1, 2, 3, 4, 5))
            mm(PS[pb][:, :], wsm[:, 0, gi, :], pooled[:, gi, a:b], True, True,
               T("wsm", 0) + T("pooled", (gi, 0 if ti == 0 else 1)), T("ps", pb))
            act(hT[:, gi, a:b], PS[pb][:, :], AF.Identity, T("ps", pb) + T("vT", 1), T("hT", (ti, gi)), scale=pscT[:, gi:gi + 1])

    for g0_ in range(2):
        dma("sp", wview(4 + g0_, 1, [8, 512]),
            t4096_d[0, 0, g0_ * 1024:(g0_ + 1) * 1024, :].rearrange("(c p) k -> p c k", p=128),
            (), T("wpg", 4 + g0_), ("wpg", 4 + g0_))
    UB_PARTS = [(0, 6, [0]), (6, 12, [0, 1]), (12, 18, [1, 2]), (18, 32, [2, 3])]
    for pi, (n0, n1, grps) in enumerate(UB_PARTS):
        alias = []
        for g_ in grps:
            alias += T("Upad", *[(g_, s_) for s_ in range(5)])
        if pi == 3:
            alias += T("Sa") + T("Sb")
        dma("sp", uBall[:, n0:n1, :],
            ag_ub_out.ap()[n0 * 128:n1 * 128, :].rearrange("(c p) f -> p c f", p=128),
            T("ag_ub_out"), T("uBall", pi) + alias, ("uBall", pi))

    def ub_part(n_):
        return 0 if n_ < 6 else (1 if n_ < 12 else (2 if n_ < 18 else 3))
    TD = [wview(4, 1, [8, 512]), wview(5, 1, [8, 512])]
    gcnt = 0
    for kt in range(2):
        for cs in range(2):
            for ng in range(4):
                pg = 4 + gcnt % 2
                if gcnt >= 2:
                    dma("sp", TD[gcnt % 2], t4096_d[kt, cs, ng * 1024:(ng + 1) * 1024, :].rearrange("(c p) k -> p c k", p=128),
                        (), T("wpg", pg), ("wpg", pg))
                for j in range(8):
                    nchunk = ng * 8 + j
                    for g in range(4):
                        mm(PS[g][:, :], uBall[:, nchunk, g * 128:(g + 1) * 128], TD[gcnt % 2][:, j, :],
                           nchunk == 0, nchunk == 31, T("uBall", ub_part(nchunk)) + T("wpg", pg), T("ps", g))
                gcnt += 1
                if gcnt in (4, 8, 12):
                    pj = 3 + (gcnt // 4 - 1)
                    adaln(0, [pj], force_p0=0, phase="mm")
                    if pj < 5:
                        adaln(0, [pj + 1], force_p0=0, phase="dma")
            for g in range(4):
                cp("act" if g % 2 == 0 else "dve", Pp[:, g, cs, :], PS[g][:, :], T("ps", g), T("Pp", (g, cs, 0), (g, cs, 1)))
        ti = 1 + kt
        a, b, c = TILES[ti]
        for g in range(4):
            pb = nextps((4, 5, 6, 7))
            for cs in range(2):
                mm(PS[pb][:, :], wsm[:, 1 + cs, g, :], Pp[:, g, cs, :], cs == 0, cs == 1,
                   T("wsm", 1 + cs) + T("Pp", (g, cs, 0), (g, cs, 1)), T("ps", pb))
            cp("act", hT[:, 4 + g, a:b], PS[pb][:, :], T("ps", pb), T("hT", (ti, 4 + g)))

    def resid_update(l, which, ti, m, pb):
        a, b, c = TILES[ti]
        stt(xT[:, m, a:b], PS[pb][:, :], Gcoef(l, which, m, c), xT[:, m, a:b], ALU.mult, ALU.add,
            T("ps", pb) + modtoks(l, which, c) + xtile_toks(ti), T("xT", ("t", ti)))

    if debug_stage == "mix":
        for ti in range(3):
            a, b, c = TILES[ti]
            for k in range(KC):
                cp("dve", xT[:, k, a:b], hT[:, k, a:b], T("hT", (ti, k)) + xtile_toks(ti), T("xT", ("t", ti)))
    for ti in range(3):
        if debug_stage == "mix":
            break
        a, b, c = TILES[ti]
        for m in range(8):
            pb = nextps((0, 1, 2, 3, 4, 5))
            for k in range(KC):
                mm(PS[pb][:, :], WO[:, k, m * 128:(m + 1) * 128], hT[:, k, a:b], k == 0, k == KC - 1,
                   T("hT", (ti, k)) + T("wpg", 2, 3), T("ps", pb))
            resid_update(0, 0, ti, m, pb)
        if debug_stage is None and ti >= 1:
            norm(0, 1, ti - 1, mod_eng="pool" if ti % 2 == 1 else "act")
    if debug_stage is None:
        norm(0, 1, 2, mod_eng="act")


    S.retire(["uBall", "uBp", "uBst", "Pp"], ["hid", "relu"])
    hid = scr_bf(0, [2, 4, 512])
    relu_f = scr_f32(2 * 4 * 512, [2, 512])

    def mlp(l, nst=3, per_fg=None, do_norm=True, tile_done=None):
        for ti in range(3):
            if do_norm:
                norm(l, 1, ti, mod_eng="pool" if ti % 2 == 0 else "act")
        NFG = 8
        stage = [0]

        def load_w(fg):
            s = fg % nst
            w1 = wview(2 * s, 1, [KC, 512])
            w2 = wview(2 * s + 1, 1, [4, 1024])
            dma("pool", w1, wm1_d[l, :, fg * 512:(fg + 1) * 512].rearrange("(k p) f -> p k f", p=128),
                (), T("wpg", 2 * s), ("wpg", 2 * s))
            dma("pool", w2, wm2_d[l, fg * 512:(fg + 1) * 512, :].rearrange("(j p) d -> p j d", p=128),
                (), T("wpg", 2 * s + 1), ("wpg", 2 * s + 1))
            return w1, w2

        ws = {}
        ws[0] = load_w(0)
        if nst == 3:
            ws[1] = load_w(1)
        hcnt = [0]

        def mlp1(fg, ti):
            a, b, c = TILES[ti]
            w1, _ = ws[fg]
            hs = hcnt[0] % 2
            hcnt[0] += 1
            for j in range(4):
                pb = nextps((0, 1, 2, 3))
                for k in range(KC):
                    mm(PS[pb][:, :], w1[:, k, j * 128:(j + 1) * 128], hT[:, k, a:b], k == 0, k == KC - 1,
                       T("hT", (ti, k)) + T("wpg", 2 * (fg % nst)), T("ps", pb))
                rs = j % 2
                act(relu_f[:, rs, :], PS[pb][:, :], AF.Relu, T("ps", pb), T("relu", rs))
                stt(hid[:, hs, j, :], relu_f[:, rs, :], 1.0, PS[pb][:, :], ALU.mult, ALU.mult,
                    T("relu", rs) + T("ps", pb), T("hid", (hs, j)))
            return hs

        def mlp2(fg, ti, hs):
            _, w2 = ws[fg]
            a, b, c = TILES[ti]
            for m in range(8):
                pb = nextps((4, 5, 6, 7))
                for j in range(4):
                    mm(PS[pb][:, :], w2[:, j, m * 128:(m + 1) * 128], hid[:, hs, j, :], j == 0, j == 3,
                       T("hid", (hs, j)) + T("wpg", 2 * (fg % nst) + 1), T("ps", pb))
                resid_update(l, 1, ti, m, pb)

        pend = None
        for fg in range(NFG):
            for ti in range(3):
                hs = mlp1(fg, ti)
                if pend is not None:
                    mlp2(*pend)
                    if tile_done is not None and pend[0] == NFG - 1:
                        tile_done(pend[1])
                pend = (fg, ti, hs)
                if nst == 3:
                    if ti == 0 and fg >= 1 and fg + 1 < NFG:
                        ws[fg + 1] = load_w(fg + 1)
                else:
                    if ti == 0 and fg + 1 < NFG:
                        ws[fg + 1] = load_w(fg + 1)
                        if per_fg is not None:
                            per_fg(fg, "dma")
                    if ti == 2 and fg + 1 < NFG and per_fg is not None:
                        per_fg(fg, "mm")
        mlp2(*pend)
        if tile_done is not None:
            tile_done(pend[1])

    ostage = scr_f32(8192, [2, D])

    def store_out(dst, col0, chunks):
        for cidx in chunks:
            sl = cidx % 2
            c0 = col0 + cidx * 128
            ti = c0 // 512
            for half in range(2):
                pb = nextps((0, 1, 2, 3))
                for j in range(4):
                    k = half * 4 + j
                    tr(PS[pb][:, j * 128:(j + 1) * 128], xT[:, k, c0:c0 + 128], identf[:],
                       xtile_toks(ti) + T("identf"), T("ps", pb))
                cp("act" if half == 0 else "dve", ostage[:, sl, half * 512:(half + 1) * 512], PS[pb][:, :],
                   T("ps", pb), T("ostage", (sl, half)))
            dma("sp", dst[cidx * 128:(cidx + 1) * 128, :], ostage[:, sl, :], T("ostage", (sl, 0), (sl, 1)),
                T("outd", (col0, cidx)), ("ostage", sl))

    def store_tile(ti):
        if ti == 0:
            store_out(yp_d, 0, range(4))
        else:
            store_out(ys_d, 512, range((ti - 1) * 4, (ti - 1) * 4 + 4))

    if debug_stage not in ("mix", "xmix"):
        if debug_stage is None:
            mlp(0, nst=2, per_fg=lambda fg, ph: adaln(1, [fg], force_p0=4, phase=ph) if fg < 6 else None, do_norm=False,
                tile_done=lambda ti: norm(1, 0, ti, mod_eng="pool" if ti % 2 == 0 else "act"))
        else:
            mlp(0)
    L1_OUT = []
    if debug_stage is None:
        LAM_INIT = 0.8 - 0.6 * math.exp(-0.3 * 1)
        wio_d = din("w_in_odd", [D, 2304])
        woo_d = din("w_out_odd", [D, D])
        gq_d = [din(nm, [64]) for nm in ("c_qn_g", "c_kn_g", "d_qn_g", "d_kn_g")]
        lam4_d = din("lam4", [4, 64])
        subln_d = din("c_subln_g", [128])
        sink_d = din("d_sink", [8])
        cck_d = din("cache_c_k", [512, 512])
        ccv_d = din("cache_c_v", [512, 512])
        cdk_d = din("cache_d_k", [512, 128])
        cdv_d = din("cache_d_v", [512, 128])
        rope_d = din("rope", [1024, 2, 64])
        band_d = din("band", [128, 2, 128], BF16)
        mg_d = din("mg", [128, 2, 4, 128], BF16)
        nck_d = dout("new_c_k", [512, 512])
        ncv_d = dout("new_c_v", [512, 512])
        ndk_d = dout("new_d_k", [512, 128])
        ndv_d = dout("new_d_v", [512, 128])
        ag_ck_in = nc.dram_tensor("ag_ck_in", [512, 1024], BF16)
        ag_ck_out = nc.dram_tensor("ag_ck_out", [2048, 1024], BF16)
        ag_cv_in = nc.dram_tensor("ag_cv_in", [1024, 512], BF16)
        ag_cv_out = nc.dram_tensor("ag_cv_out", [4096, 512], BF16)
        ag_b_in = nc.dram_tensor("ag_b_in", [256, 384], BF16)
        ag_b_out = nc.dram_tensor("ag_b_out", [1024, 384], BF16)

        S.retire(["t256"], ["mg"])
        S.retire(["dftc"], ["band"])
        S.retire(["vecs"], ["scs"])
        mg = t256[:, :, :, :].rearrange("p a b k -> p (a b k)").rearrange("p (s r t) -> p s r t", s=2, r=4)
        band = dftc
        scs = vecs[:, 0:64]
        wpgf = wpg[:, :, :].rearrange("p a b -> p (a b)")

        def wflat(off, shape):
            n = int(np.prod(shape))
            v = wpgf[:, off:off + n]
            if len(shape) == 1:
                return v
            names = " ".join("d%d" % i for i in range(len(shape)))
            kw = {"d%d" % i: s for i, s in enumerate(shape[:-1])}
            return v.rearrange("p (%s) -> p %s" % (names, names), **kw)

        WQ = wflat(0, [KC, 2304])
        KTb = [wflat(0, [4096]), wflat(8224, [4096])]
        Vb = [wflat(4096, [32, 129]), wflat(12320, [32, 129])]
        WOO = wflat(0, [KC, 1024])
        KTctx = wflat(18920, [4, 512])
        Vctx = wflat(20968, [4, 4, 129])
        dkTctx = wflat(23032, [2, 512])
        Vdctx = wflat(24056, [4, 2, 65])

        qs = scr_f32(0, [2, 2304])
        tmpsq = scr_f32(9216, [1024])
        rt2 = scr_f32(11264, [1024])
        dkdup = scr_f32(13312, [2, 2, 2, 64])
        ckst = scr_bf(14336, [2, 4, 128])
        cvst = scr_bf(15360, [2, 512])
        ssq = scr_f32(16384, [2, 32])
        rsq = scr_f32(16512, [2, 32])
        ropeT = scr_f32(16640, [2, 2, 64])
        cqT_p = scr_bf(18432, [4, 512])
        dqT_p = scr_bf(20480, [4, 512])
        ckT_p = scr_bf(22528, [4, 512])
        Vc_p = scr_bf(24576, [4, 4, 129])
        dkT_p = scr_bf(26640, [2, 512])
        Vd_p = scr_bf(27664, [4, 2, 65])
        cqT_s = scr_bf(18432, [4, 1024])
        dqT_s = scr_bf(22528, [4, 1024])
        dkT_own = scr_bf(26624, [2, 1024])
        Vd_own = scr_bf(28672, [8, 2, 65])
        dkT_g = scr_bf(0, [4, 2, 2, 128])
        Vd_g = scr_bf(2048, [8, 2, 65])

        L0_SCR = ["hid", "relu", "uBall", "uBp", "uBst", "Pp", "pooled", "Upad", "Sa", "Sb", "xstage"]
        QKV_SCR = ["qs", "tmpsq", "rt2", "dkdup", "ckst", "cvst", "ssq", "rsq", "ropeT"]
        P_SCR = ["cqT_p", "dqT_p", "ckT_p", "Vc_p", "dkT_p", "Vd_p"]
        S_SCR = ["cqT_s", "dqT_s", "dkT_own", "Vd_own"]
        G_SCR = ["dkT_g", "Vd_g"]

        S.retire(["wpg"], ["WQ", "KTctx", "Vctx", "dkTctx", "Vdctx"])
        S.retire(L0_SCR, QKV_SCR + P_SCR + ["sqA", "sqB"])
        S.retire(["tmpf"], ["o1", "osq", "cout"])
        S.retire(["rstd"], ["dout"])
        o1 = tmpf[:, 0, :].rearrange("p (q e) -> p q e", q=4)
        osq = tmpf[:, 1, 0:128]
        cout = tmpf[:, 1, 128:384].rearrange("p (s e) -> p s e", s=2)
        dout_s = rstd

        for half in range(2):
            dma("pool", WQ[:, :, half * 1152:(half + 1) * 1152],
                wio_d[:, half * 1152:(half + 1) * 1152].rearrange("(k p) c -> p k c", p=128), (), T("WQ", half), ("WQ", half))
        CK1 = "const1"
        for t in range(4):
            dma("sp", gsm[:, t, :], gq_d[t].partition_broadcast(128), (), T("gsm", t), CK1, wait_all=True)
        dma("sp", lamv[:], lam4_d.partition_broadcast(128), (), T("lamv"), CK1, wait_all=True)
        dma("sp", gsub[:, 0:1], subln_d.rearrange("(p o) -> p o", o=1), (), T("gsub", 0), CK1, wait_all=True)
        memset("dve", ones1[:], 1.0, T("ones1"))
        dma("sp", es8[:], sink_d.partition_broadcast(128), (), T("es8", 0), CK1, wait_all=True)
        dma("sp", band[:], band_d, (), T("band"), CK1, wait_all=True)
        dma("sp", mg[:], mg_d, (), T("mg"), CK1, wait_all=True)
        ts("pool", gsub[:, 1:2], gsub[:, 0:1], 1.0 - LAM_INIT, 0.0, ALU.mult, ALU.add, T("gsub", 0), T("gsub", 1))
        act(es8[:], es8[:], AF.Exp, T("es8", 0), T("es8", 1))
        lprod = osq.rearrange("p (a d) -> p a d", a=2)
        lv = lamv[:].rearrange("p (a b) d -> p a b d", b=2)
        tt("dve", lprod, lv[:, :, 0, :], lv[:, :, 1, :], ALU.mult, T("lamv"), T("osq"))
        S.add("dve", lambda e: e.tensor_reduce(lamw[:, 0:2], lprod, AX.X, ALU.add), r=T("osq"), w=T("lamw", 0))
        act(lamw[:, 2:4], lamw[:, 0:2], AF.Exp, T("lamw", 0), T("lamw", 1))
        tt("dve", lamw[:, 4:5], lamw[:, 3:4], lamw[:, 2:3], ALU.subtract, T("lamw", 1), T("lamw", 2))
        ts("dve", lamw[:, 5:6], lamw[:, 4:5], -LAM_INIT, None, ALU.add, None, T("lamw", 2), T("lamw", 3))
        NEGLAM = lamw[:, 5:6]

        scnt = [0]

        def nexts():
            i = scnt[0] % 64
            scnt[0] += 1
            return i

        dkcnt = [0]

        def dk_post(src, srctoks, dst, dsttoks):
            sl = dkcnt[0] % 2
            dkcnt[0] += 1
            cp("pool", dkdup[:, sl, :, :, :], src.unsqueeze(2).to_broadcast([128, 2, 2, 64]), srctoks, T("dkdup", sl))
            pb = nextps((5, 6, 7))
            for j in range(2):
                tr(PS[pb][:, j * 128:(j + 1) * 128], dkdup[:, sl, j, :, :].rearrange("p u d -> p (u d)"), identf[:],
                   T("dkdup", sl) + T("identf"), T("ps", pb))
            cp("dve", dst, PS[pb][:, 0:256].rearrange("p (j t) -> p j t", j=2), T("ps", pb), dsttoks)

        for kc in range(4):
            sl = kc % 2
            dma("sp", qs[:, sl, 0:512], cck_d[kc * 128:(kc + 1) * 128, :], (), T("qs", (sl, "n1")), ("qs", sl, 1))
            dma("sp", qs[:, sl, 2048:2176], cdk_d[kc * 128:(kc + 1) * 128, :], (), T("qs", (sl, "n2")), ("qs", sl, 2))
            pb = nextps((5, 6, 7))
            for h in range(4):
                tr(PS[pb][:, h * 128:(h + 1) * 128], qs[:, sl, h * 128:(h + 1) * 128], identf[:], T("qs", (sl, "n1")) + T("identf"), T("ps", pb))
            cp("act", KTctx[:, :, kc * 128:(kc + 1) * 128], PS[pb][:, :].rearrange("p (h t) -> p h t", h=4), T("ps", pb), T("KTctx", kc))
            dk_post(qs[:, sl, 2048:2176].rearrange("p (j d) -> p j d", j=2), T("qs", (sl, "n2")),
                    dkTctx[:, :, kc * 128:(kc + 1) * 128], T("dkTctx", kc))
        for kc in range(4):
            dma("pool", Vctx[:, kc, :, 0:128], ccv_d[kc * 128:(kc + 1) * 128, :].rearrange("p (h e) -> p h e", h=4), (),
                T("Vctx", 0), "Vctx")
        memset("dve", Vctx[:, :, :, 128:129], 1.0, T("Vctx", 1))
        for kc in range(4):
            dma("pool", Vdctx[:, kc, :, 0:64], cdv_d[kc * 128:(kc + 1) * 128, :].rearrange("p (j d) -> p j d", j=2), (),
                T("Vdctx", 0), "Vdctx")
        memset("dve", Vdctx[:, :, :, 64:65], 1.0, T("Vdctx", 1))
        KTctx_t = T("KTctx", 0, 1, 2, 3)
        dkTctx_t = T("dkTctx", 0, 1, 2, 3)

        SEGS = [(0, 1024, 16, 0), (1536, 2176, 10, 16)]

        sqA = scr_bf(29712, [1024])
        sqB = scr_bf(30736, [640])
        SQ_SCR = ["sqA", "sqB"]

        def qkv_mm(cidx):
            ti = 0 if cidx < 4 else 1 + (cidx - 4) // 4
            a = cidx * 128
            for nt in range(5):
                wd = 512 if nt < 4 else 256
                for k in range(KC):
                    mm(PS[nt][:, 0:wd], hT[:, k, a:a + 128], WQ[:, k, nt * 512:nt * 512 + wd], k == 0, k == KC - 1,
                       T("hT", (ti, k)) + T("WQ", 0, 1), T("ps", nt))

        def qkv_post(cidx):
            slot = cidx % 2
            QN1, QV, QN2 = T("qs", (slot, "n1")), T("qs", (slot, "v")), T("qs", (slot, "n2"))
            tokmap = [QN1, QN1, QV, QN2, QN2]
            act(sqA[:, 0:512], PS[0][:, 0:512], AF.Square, T("ps", 0), T("sqA", 0))
            act(sqA[:, 512:1024], PS[1][:, 0:512], AF.Square, T("ps", 1), T("sqA", 1))
            act(sqB[:, 0:512], PS[3][:, 0:512], AF.Square, T("ps", 3), T("sqB", 0))
            act(sqB[:, 512:640], PS[4][:, 0:128], AF.Square, T("ps", 4), T("sqB", 1))
            for nt in range(5):
                wd = 512 if nt < 4 else 256
                cp("act", qs[:, slot, nt * 512:nt * 512 + wd], PS[nt][:, 0:wd], T("ps", nt), tokmap[nt])

        def qkv_stats(cidx):
            slot = cidx % 2
            QN1, QV, QN2 = T("qs", (slot, "n1")), T("qs", (slot, "v")), T("qs", (slot, "n2"))
            S.add("dve", lambda e, o=ssq[:, slot, 0:16], i=sqA[:, 0:1024].rearrange("p (g d) -> p g d", d=64):
                  e.tensor_reduce(o, i, AX.X, ALU.add), r=T("sqA", 0, 1), w=T("ssq", (slot, 0)))
            S.add("dve", lambda e, o=ssq[:, slot, 16:26], i=sqB[:, 0:640].rearrange("p (g d) -> p g d", d=64):
                  e.tensor_reduce(o, i, AX.X, ALU.add), r=T("sqB", 0, 1), w=T("ssq", (slot, 16)))
            act(rsq[:, slot, 0:26], ssq[:, slot, 0:26], AF.Ln, T("ssq", (slot, 0), (slot, 16)) + T("epsc"), T("rsq", slot),
                bias=epsc[:, 0:1], scale=1.0 / 64)
            act(rsq[:, slot, 0:26], rsq[:, slot, 0:26], AF.Exp, T("rsq", slot), T("rsq", slot), scale=-0.5)
            v = qs[:, slot, 0:1024].rearrange("p (t g d) -> p t g d", t=2, d=64)
            tt("pool", v, v, gsm[:, 0:2, :].unsqueeze(2).to_broadcast([128, 2, 8, 64]), ALU.mult, QN1 + T("gsm", 0, 1), QN1)
            for (c0, ng, t) in [(1536, 8, 2), (2048, 2, 3)]:
                v = qs[:, slot, c0:c0 + ng * 64].rearrange("p (g d) -> p g d", d=64)
                tt("pool", v, v, gsm[:, t:t + 1, :].to_broadcast([128, ng, 64]), ALU.mult, QN2 + T("gsm", t), QN2)

        def qkv_s2(cidx):
            sample = cidx >= 4
            slot = cidx % 2
            cs = cidx - 4
            QN1, QV, QN2 = T("qs", (slot, "n1")), T("qs", (slot, "v")), T("qs", (slot, "n2"))
            Q = QN1 + QV + QN2
            SEG2 = [(0, 1024, 16, 0, QN1), (1536, 2176, 10, 16, QN2)]
            if sample:
                dma("sp", ropeT[:, slot, :, :], rope_d[cs * 128:(cs + 1) * 128, :, :], (), T("ropeT", slot), ("ropeT", slot))
                for (c0, c1, ng, go, QN) in SEG2:
                    n = c1 - c0
                    v = qs[:, slot, c0:c1].rearrange("p (g d) -> p g d", d=64)
                    t1 = tmpsq[:, 0:n].rearrange("p (g d) -> p g d", d=64)
                    t2 = rt2[:, 0:n].rearrange("p (g d) -> p g d", d=64)
                    tt("dve", t1, v, ropeT[:, slot, 0:1, :].to_broadcast([128, ng, 64]), ALU.mult,
                       QN + T("ropeT", slot), T("tmpsq"))
                    v5 = qs[:, slot, c0:c1].rearrange("p (g a h f) -> p g a h f", a=2, h=2, f=16)
                    t5 = rt2[:, 0:n].rearrange("p (g a h f) -> p g a h f", a=2, h=2, f=16)
                    s4 = ropeT[:, slot, 1, :].rearrange("p (a h f) -> p a h f", a=2, h=2)
                    for hf in range(2):
                        tt("pool", t5[:, :, :, hf, :], v5[:, :, :, 1 - hf, :],
                           s4[:, :, hf, :].unsqueeze(1).to_broadcast([128, ng, 2, 16]), ALU.mult,
                           QN + T("ropeT", slot), T("rt2"))
                    tt("dve", v, t1, t2, ALU.add, T("tmpsq") + T("rt2"), QN)
            for (c0, c1, ng, go, QN) in SEG2:
                v = qs[:, slot, c0:c1].rearrange("p (g d) -> p g d", d=64)
                tt("dve", v, v, rsq[:, slot, go:go + ng].unsqueeze(2).to_broadcast([128, ng, 64]), ALU.mult,
                   QN + T("rsq", slot), QN)
            if not sample:
                r0 = cidx * 128
                dma("sp", nck_d[r0:r0 + 128, :], qs[:, slot, 512:1024], QN1, T("o_nck", cidx), ("qso", slot, 0))
                dma("sp", ncv_d[r0:r0 + 128, :], qs[:, slot, 1024:1536], QV, T("o_ncv", cidx), ("qso", slot, 1))
                dma("sp", ndk_d[r0:r0 + 128, :], qs[:, slot, 2048:2176], QN2, T("o_ndk", cidx), ("qso", slot, 2))
                dma("sp", ndv_d[r0:r0 + 128, :], qs[:, slot, 2176:2304], QN2, T("o_ndv", cidx), ("qso", slot, 3))

            def tr4(c0, QN):
                pb = nextps((5, 6, 7))
                for j in range(4):
                    tr(PS[pb][:, j * 128:(j + 1) * 128], qs[:, slot, c0 + j * 128:c0 + (j + 1) * 128], identf[:],
                       QN + T("identf"), T("ps", pb))
                return pb

            cqT = cqT_s if sample else cqT_p
            dqT = dqT_s if sample else dqT_p
            tc0 = cs * 128 if sample else cidx * 128
            cqn = "cqT_s" if sample else "cqT_p"
            dqn = "dqT_s" if sample else "dqT_p"
            pb = tr4(0, QN1)
            cp("dve", cqT[:, :, tc0:tc0 + 128], PS[pb][:, :].rearrange("p (h t) -> p h t", h=4), T("ps", pb), T(cqn, tc0))
            pb = tr4(1536, QN2)
            cp("act", dqT[:, :, tc0:tc0 + 128], PS[pb][:, :].rearrange("p (h t) -> p h t", h=4), T("ps", pb), T(dqn, tc0))
            pb = tr4(512, QN1)
            if sample:
                cp("dve", ckst[:, slot, :, :], PS[pb][:, :].rearrange("p (h t) -> p h t", h=4), T("ps", pb), T("ckst", slot))
                dma("sp", ag_ck_in.ap().rearrange("(h p) t -> p h t", p=128)[:, :, cs * 128:(cs + 1) * 128], ckst[:, slot, :, :],
                    T("ckst", slot), T("ag_ck_in", cs), ("ckst", slot))
                cp("act", cvst[:, slot, :], qs[:, slot, 1024:1536], QV, T("cvst", slot))
                dma("sp", ag_cv_in.ap()[cs * 128:(cs + 1) * 128, :], cvst[:, slot, :], T("cvst", slot), T("ag_cv_in", cs),
                    ("cvst", slot))
                dk_post(qs[:, slot, 2048:2176].rearrange("p (j d) -> p j d", j=2), QN2,
                        dkT_own[:, :, cs * 128:(cs + 1) * 128], T("dkT_own", cs))
                cp("act", Vd_own[:, cs, :, 0:64], qs[:, slot, 2176:2304].rearrange("p (j d) -> p j d", j=2), QN2, T("Vd_own", cs))
                if cs in (0, 7):
                    side = 0 if cs == 0 else 1
                    dma("sp", ag_b_in.ap()[:, 0:256].rearrange("(j p) (s t) -> p j s t", p=128, s=2)[:, :, side, :],
                        dkT_own[:, :, cs * 128:(cs + 1) * 128], T("dkT_own", cs), T("ag_b_in", (side, 0)), ("agb", side))
                    dma("sp", ag_b_in.ap()[side * 128:(side + 1) * 128, 256:384].rearrange("t (j d) -> t j d", j=2),
                        Vd_own[:, cs, :, 0:64], T("Vd_own", cs), T("ag_b_in", (side, 1)), ("agb", side))
            else:
                cp("dve", ckT_p[:, :, tc0:tc0 + 128], PS[pb][:, :].rearrange("p (h t) -> p h t", h=4), T("ps", pb), T("ckT_p", cidx))
                cp("act", Vc_p[:, cidx, :, 0:128], qs[:, slot, 1024:1536].rearrange("p (h e) -> p h e", h=4), QV, T("Vc_p", cidx))
                dk_post(qs[:, slot, 2048:2176].rearrange("p (j d) -> p j d", j=2), QN2,
                        dkT_p[:, :, tc0:tc0 + 128], T("dkT_p", cidx))
                cp("act", Vd_p[:, cidx, :, 0:64], qs[:, slot, 2176:2304].rearrange("p (j d) -> p j d", j=2), QN2, T("Vd_p", cidx))

        def qkv_run(cids):
            prev = None
            for cidx in cids:
                qkv_mm(cidx)
                qkv_post(cidx)
                if prev is not None:
                    qkv_s2(prev)
                qkv_stats(cidx)
                prev = cidx
            qkv_s2(prev)

        pcnt = [0]
        ccnt = [0]
        dcnt = [0]

        ATT = ["pTr", "gsum", "rden", "o1T", "sqT", "QZ", "tT"]
        pTr = scr_bf(4096, [8, 512])
        gsum = scr_bf(8192, [2, 512])
        rden = scr_f32(9216, [512])
        o1T = scr_f32(10240, [512])
        sqT = scr_bf(11264, [512])
        QZ = scr_bf(12288, [2, 1024])
        tT = scr_f32(14336, [512])
        stc = [0]
        gcnt2 = [0]
        acct = [0]
        dpar = [0]

        def att_begin():
            att_begin_w()
            memset("pool", QZ[64:128, 0, :], 0.0, T("QZ", "z0"))
            memset("pool", QZ[0:64, 1, :], 0.0, T("QZ", "z1"))

        def diff_attend(h, cqT, qtoks, qa, qb_, chunks, hcol0, ti):
            n = qb_ - qa
            nch = len(chunks)
            par = 0
            if n <= 256:
                par = dpar[0] % 2
                dpar[0] += 1
            eo = par * 256
            qo = par * 512
            for c in range(2):
                cp("pool", QZ[c * 64:(c + 1) * 64, c, qo:qo + n], cqT[c * 64:(c + 1) * 64, h, qa:qb_], qtoks + T("QZ", "z%d" % c),
                   T("QZ", (c, par)))
            for c in range(2):
                bO, bD = ((0, 1), (2, 3))[acct[0] % 2]
                acct[0] += 1

                def st(ci, c=c):
                    kT, v, toks = chunks[ci]
                    pb = (4, 5, 6, 7)[stc[0] % 4]
                    stc[0] += 1
                    mm(PS[pb][:, 0:n], kT, QZ[:, c, qo:qo + n], True, True, toks + T("QZ", (c, par), "z%d" % c), T("ps", pb))
                    return pb
                LOOK = 3
                pbs = {}
                for ci in range(min(LOOK, nch)):
                    pbs[ci] = st(ci)
                grp = []
                pend = []
                ngrp = (nch + 3) // 4
                gi = 0
                for ci in range(nch):
                    if ci + LOOK < nch:
                        pbs[ci + LOOK] = st(ci + LOOK)
                    kT, v, toks = chunks[ci]
                    pb = pbs.pop(ci)
                    sl = pcnt[0] % 8
                    pcnt[0] += 1
                    act(pTr[:, sl, 0:n], PS[pb][:, 0:n], AF.Exp, T("ps", pb), T("pTr", sl), scale=0.125)
                    mm(PS[bO][:, 0:n], v[:, 0:128], pTr[:, sl, 0:n], ci == 0, ci == nch - 1, T("pTr", sl) + toks, T("ps", bO))
                    grp.append(sl)
                    if len(grp) == 4 or ci == nch - 1:
                        if len(grp) == 1:
                            src, srct = pTr[:, grp[0], 0:n], T("pTr", grp[0])
                        else:
                            gs = gcnt2[0] % 2
                            gcnt2[0] += 1
                            tt("dve", gsum[:, gs, 0:n], pTr[:, grp[0], 0:n], pTr[:, grp[1], 0:n], ALU.add,
                               T("pTr", grp[0], grp[1]), T("gsum", gs))
                            for s_ in grp[2:]:
                                tt("dve", gsum[:, gs, 0:n], gsum[:, gs, 0:n], pTr[:, s_, 0:n], ALU.add,
                                   T("gsum", gs) + T("pTr", s_), T("gsum", gs))
                            src, srct = gsum[:, gs, 0:n], T("gsum", gs)
                        pend.append((ci + 3, src, srct, gi == 0, gi == ngrp - 1))
                        gi += 1
                        grp = []
                    while pend and (pend[0][0] <= ci or ci == nch - 1):
                        _, src_, srct_, f_, l_ = pend.pop(0)
                        mm(PS[bD][:, 0:n], ones1[:], src_, f_, l_, srct_ + T("ones1"), T("ps", bD))
                    if ci < nch - 1:
                        yield
                def epi(c=c, bO=bO, bD=bD, n=n, eo=eo, par=par, h=h, hcol0=hcol0, ti=ti):
                    act(rden[:, eo:eo + n], PS[bD][:, 0:n], AF.Ln, T("ps", bD), T("rden", par))
                    act(rden[:, eo:eo + n], rden[:, eo:eo + n], AF.Exp, T("rden", par), T("rden", par), scale=-1.0)
                    if c == 0:
                        tt("dve", o1T[:, eo:eo + n], PS[bO][:, 0:n], rden[:, eo:eo + n], ALU.mult, T("ps", bO) + T("rden", par), T("o1T", par))
                    else:
                        stt(tT[:, eo:eo + n], PS[bO][:, 0:n], NEGLAM, rden[:, eo:eo + n], ALU.mult, ALU.mult,
                            T("ps", bO) + T("rden", par) + T("lamw", 3), T("tT", par))
                        tt("dve", o1T[:, eo:eo + n], tT[:, eo:eo + n], o1T[:, eo:eo + n], ALU.add, T("tT", par) + T("o1T", par), T("o1T", par))
                        tt("pool", sqT[:, eo:eo + n], o1T[:, eo:eo + n], o1T[:, eo:eo + n], ALU.mult, T("o1T", par), T("sqT", par))
                        mm(PS[bD][:, 0:n], ones1[:], sqT[:, eo:eo + n], True, True, T("sqT", par) + T("ones1"), T("ps", bD))
                        act(rden[:, eo:eo + n], PS[bD][:, 0:n], AF.Ln, T("ps", bD) + T("epsc"), T("rden", par), bias=epsc[:, 0:1], scale=1.0 / 128)
                        act(rden[:, eo:eo + n], rden[:, eo:eo + n], AF.Exp, T("rden", par), T("rden", par), scale=-0.5)
                        stt(hT[:, h, hcol0:hcol0 + n], o1T[:, eo:eo + n], gsub[:, 1:2], rden[:, eo:eo + n], ALU.mult, ALU.mult,
                            T("o1T", par) + T("rden", par) + T("gsub", 1), T("hT", (ti, h)))
                flush_epi()
                epi_q.append(epi)
                yield

        epi_q = []

        def flush_epi():
            while epi_q:
                epi_q.pop(0)()

        QZw = scr_bf(15360, [2, 2, 2, 128])
        ATT.append("QZw")
        wcnt = [0]
        wfin = [0]
        wstc = [0]
        wpc = [0]

        def att_begin_w():
            memset("pool", QZw[64:128, :, :, 0, :], 0.0, T("QZw", "z0"))
            memset("pool", QZw[0:64, :, :, 1, :], 0.0, T("QZw", "z1"))

        def win_run(dqT, qtoks, blocks):
            units = [(bi_, j) for bi_ in range(len(blocks)) for j in range(2)]

            def prep(u):
                bi_, j = units[u]
                qa = blocks[bi_][0]
                ws = u % 2
                for hf in range(2):
                    cp("dve", QZw[hf * 64:(hf + 1) * 64, ws, :, hf, :], dqT[hf * 64:(hf + 1) * 64, j * 2:j * 2 + 2, qa:qa + 128],
                       qtoks + T("QZw", "z%d" % hf), T("QZw", (ws, hf)))
            prep(0)
            for u, (bi_, j) in enumerate(units):
                qa, chunks_fn, hcol, ti = blocks[bi_]
                if u + 1 < len(units):
                    prep(u + 1)
                ws = u % 2
                accb = (3, 1)[u % 2]
                ds_ = bi_ % 2
                chunks = chunks_fn(j)
                nch = len(chunks)
                qz = QZw[:, ws, :, :, :].rearrange("p a b q -> p (a b q)")
                qzt = T("QZw", (ws, 0), (ws, 1), "z0", "z1")

                def st(ci, chunks=chunks, qz=qz, qzt=qzt):
                    kT, v, toks, mask, mtoks = chunks[ci]
                    pb = (0, 2, 6, 7)[wstc[0] % 4]
                    wstc[0] += 1
                    mm(PS[pb][:, :], kT, qz, True, True, toks + qzt, T("ps", pb))
                    return pb
                pbs = {}
                for ci in range(min(2, nch)):
                    pbs[ci] = st(ci)
                for ci in range(nch):
                    if ci + 2 < nch:
                        pbs[ci + 2] = st(ci + 2)
                    kT, v, toks, mask, mtoks = chunks[ci]
                    pb = pbs.pop(ci)
                    sl = wpc[0] % 2
                    wpc[0] += 1
                    act(sqb[:, sl, :], PS[pb][:, :], AF.Exp, T("ps", pb), T("sqb", sl), scale=0.125)
                    if mask is not None:
                        v3 = sqb[:, sl, :].rearrange("p (g q) -> p g q", g=4)
                        tt("dve", v3, v3, mask.unsqueeze(1).to_broadcast([128, 4, 128]), ALU.mult, T("sqb", sl) + mtoks, T("sqb", sl))
                    for g in range(4):
                        mm(PS[accb][:, g * 65:(g + 1) * 65], sqb[:, sl, g * 128:(g + 1) * 128], v, ci == 0 and g == 0, ci == nch - 1,
                           T("sqb", sl) + toks, T("ps", accb), skip_group_check=True)
                accv = PS[accb][:, 0:260].rearrange("p (g e) -> p g e", e=65)
                wb = (wfin[0] % 8) * 8
                wfin[0] += 1
                tt("dve", scs[:, wb:wb + 4], accv[:, :, 64], es8[:, j * 4:j * 4 + 4], ALU.add,
                   T("ps", accb) + T("es8", 1), T("scs", ("w", wb)))
                S.add("dve", lambda e, o=scs[:, wb + 4:wb + 8], x=scs[:, wb:wb + 4]: e.reciprocal(o, x),
                      r=T("scs", ("w", wb)), w=T("scs", ("w", wb + 4)))
                tt("dve", dout_s[:, ds_, j * 256:(j + 1) * 256].rearrange("p (g d) -> p g d", g=4), accv[:, :, 0:64],
                   scs[:, wb + 4:wb + 8].unsqueeze(2).to_broadcast([128, 4, 64]), ALU.mult,
                   T("ps", accb) + T("scs", ("w", wb + 4)), T("dout", ds_))
                if j == 1:
                    pb = (4, 5)[bi_ % 2]
                    for m in range(4):
                        tr(PS[pb][:, m * 128:(m + 1) * 128], dout_s[:, ds_, m * 128:(m + 1) * 128], identf[:],
                           T("dout", ds_) + T("identf"), T("ps", pb))
                    cp("act", hT[:, 4:8, hcol:hcol + 128], PS[pb][:, :].rearrange("p (m t) -> p m t", m=4), T("ps", pb),
                       T("hT", (ti, 4), (ti, 5), (ti, 6), (ti, 7)))

        def chain(gens):
            for g_ in gens:
                yield from g_

        def interleave(main, side, ratio):
            side_alive = True
            k = 0
            for _ in main:
                k += 1
                if side_alive and k % ratio == 0:
                    try:
                        next(side)
                    except StopIteration:
                        side_alive = False
            if side_alive:
                for _ in side:
                    pass

        memset("dve", Vc_p[:, :, :, 128:129], 1.0, T("Vc_p", "ones"))
        memset("dve", Vd_p[:, :, :, 64:65], 1.0, T("Vd_p", "ones"))
        qkv_run(range(4))
        S.retire(QKV_SCR, ATT)
        att_begin()
        cqTp_t = T("cqT_p", 0, 128, 256, 384)
        dqTp_t = T("dqT_p", 0, 128, 256, 384)
        pd, pw = [], []
        for s in range(2):
            for h in range(4):
                chunks = [(ckT_p[:, h, (2 * s + kc) * 128:(2 * s + kc + 1) * 128], Vc_p[:, 2 * s + kc, h, :],
                           T("ckT_p", 2 * s + kc) + T("Vc_p", 2 * s + kc, "ones")) for kc in range(2)]
                pd.append(diff_attend(h, cqT_p, cqTp_t, s * 256, (s + 1) * 256, chunks, s * 256, 0))
            for qb in range(2):
                qa = s * 256 + qb * 128

                def chf(j, s=s):
                    return [(dkT_p[:, j, (2 * s + kc) * 128:(2 * s + kc + 1) * 128], Vd_p[:, 2 * s + kc, j, :],
                             T("dkT_p", 2 * s + kc) + T("Vd_p", 2 * s + kc, "ones"), None, []) for kc in range(2)]
                pw.append((qa, chf, qa, 0))
        for _ in chain(pd):
            pass
        flush_epi()
        win_run(dqT_p, dqTp_t, pw)

        S.retire(P_SCR, S_SCR)
        S.retire(ATT, QKV_SCR)
        memset("dve", Vd_own[:, :, :, 64:65], 1.0, T("Vd_own", "ones"))
        qkv_run(range(4, 12))
        S.add("pool", lambda e: e.collective_compute("AllGather", ALU.bypass, replica_groups=RG,
                                                     ins=[ag_b_in.ap().opt()], outs=[ag_b_out.ap().opt()]),
              r=T("ag_b_in", (0, 0), (0, 1), (1, 0), (1, 1)), w=T("ag_b_out"), dma=True, dkey="cc_b", inc=1)
        S.add("pool", lambda e: e.collective_compute("AllGather", ALU.bypass, replica_groups=RG,
                                                     ins=[ag_ck_in.ap().opt()], outs=[ag_ck_out.ap().opt()]),
              r=T("ag_ck_in", *range(8)), w=T("ag_ck_out"), dma=True, dkey="cc_ck", inc=1)
        S.add("pool", lambda e: e.collective_compute("AllGather", ALU.bypass, replica_groups=RG,
                                                     ins=[ag_cv_in.ap().opt()], outs=[ag_cv_out.ap().opt()]),
              r=T("ag_cv_in", *range(8)), w=T("ag_cv_out"), dma=True, dkey="cc_cv", inc=1)
        S.retire(["WQ"], ["KT0", "V0", "KT1", "V1"])
        S.retire(QKV_SCR, G_SCR + ATT)
        att_begin()
        for bi in range(2):
            memset("dve", Vb[bi][:, :, 128:129], 1.0, T("V%d" % bi, "ones"))
        for r in range(4):
            dma("sp", dkT_g[:, r, :, :, :],
                ag_b_out.ap()[r * 256:(r + 1) * 256, 0:256].rearrange("(j p) (s t) -> p j s t", p=128, s=2),
                T("ag_b_out"), T("dkT_g", r), ("dkT_g", r))
        for j in range(2):
            dma("sp", Vd_g[:, :, j, 0:64], ag_b_out.ap()[:, 256 + j * 64:256 + (j + 1) * 64].rearrange("(rs t) d -> t rs d", t=128),
                T("ag_b_out"), T("Vd_g", 0), "Vd_g")
        memset("dve", Vd_g[:, :, :, 64:65], 1.0, T("Vd_g", 1))
        cqTs_t = T("cqT_s", *[i * 128 for i in range(8)])
        dqTs_t = T("dqT_s", *[i * 128 for i in range(8)])

        def win_chunks(b):
            def chf(j):
                ch = []
                for kc in range(4):
                    ch.append((dkTctx[:, j, kc * 128:(kc + 1) * 128], Vdctx[:, kc, j, :],
                               T("dkTctx", kc) + T("Vdctx", 0, 1), None, []))
                for nb in (b - 1, b, b + 1):
                    if 0 <= nb <= 7:
                        mask = band[:, 0, :] if nb == b - 1 else (band[:, 1, :] if nb == b + 1 else None)
                        ch.append((dkT_own[:, j, nb * 128:(nb + 1) * 128], Vd_own[:, nb, j, :],
                                   T("dkT_own", nb) + T("Vd_own", nb, "ones"), mask, T("band") if mask is not None else []))
                if b == 0:
                    for r in range(4):
                        ch.append((dkT_g[:, r, j, 1, :], Vd_g[:, r * 2 + 1, j, :], T("dkT_g", r) + T("Vd_g", 0, 1),
                                   mg[:, 0, r, :], T("mg")))
                if b == 7:
                    for r in range(4):
                        ch.append((dkT_g[:, r, j, 0, :], Vd_g[:, r * 2 + 0, j, :], T("dkT_g", r) + T("Vd_g", 0, 1),
                                   mg[:, 1, r, :], T("mg")))
                return ch
            return chf

        win_run(dqT_s, dqTs_t, [(b * 128, win_chunks(b), 512 + b * 128, 1 + b // 4) for b in (1, 2, 3, 4, 5, 6, 0, 7)])

        def load_head(h):
            bi = h % 2
            dma("sp", KTb[bi].rearrange("p (r t) -> p r t", r=4),
                ag_ck_out.ap().rearrange("(r h p) t -> p h r t", r=4, h=4)[:, h, :, :],
                T("ag_ck_out"), T("KT%d" % bi), ("KT", bi))
            dma("sp", Vb[bi][:, :, 0:128], ag_cv_out.ap()[:, h * 128:(h + 1) * 128].rearrange("(n p) e -> p n e", p=128),
                T("ag_cv_out"), T("V%d" % bi, 0), ("V", bi))

        load_head(0)
        load_head(1)

        def diff_all():
            for h in range(4):
                bi = h % 2
                chunks = [(KTctx[:, h, kc * 128:(kc + 1) * 128], Vctx[:, kc, h, :], T("KTctx", kc) + T("Vctx", 0, 1)) for kc in range(4)]
                chunks += [(KTb[bi][:, n_ * 128:(n_ + 1) * 128], Vb[bi][:, n_, :], T("KT%d" % bi) + T("V%d" % bi, 0, "ones"))
                           for n_ in range(32)]
                for qt in range(2):
                    yield from diff_attend(h, cqT_s, cqTs_t, qt * 512, (qt + 1) * 512, chunks, 512 + qt * 512, 1 + qt)
                if h + 2 < 4:
                    load_head(h + 2)
                if h == 2:
                    S.retire(["KT0", "V0"], ["wout"])
                    dma("pool", WOO, woo_d.rearrange("(k p) c -> p k c", p=128), (), T("wout"), "wout")
        for _ in diff_all():
            pass
        flush_epi()

        S.retire(["o1", "osq", "cout"], ["tmpf"])
        S.retire(["dout"], ["rstd"])
        for ti in range(3):
            a, b, c = TILES[ti]
            for m in range(8):
                pb = nextps((0, 1, 2, 3, 4, 5))
                for k in range(KC):
                    mm(PS[pb][:, :], WOO[:, k, m * 128:(m + 1) * 128], hT[:, k, a:b], k == 0, k == KC - 1,
                       T("hT", (ti, k)) + T("wout"), T("ps", pb))
                resid_update(1, 0, ti, m, pb)
            if ti >= 1:
                norm(1, 1, ti - 1, mod_eng="pool" if ti % 2 == 1 else "act")
        norm(1, 1, 2, mod_eng="act")

        S.retire(["KT0", "V0", "KT1", "V1", "wout", "WQ", "KTctx", "Vctx", "dkTctx", "Vdctx"], ["wpg"])
        S.retire(QKV_SCR + P_SCR + S_SCR + G_SCR + L0_SCR + ATT + ["sqA", "sqB"], ["hid", "relu"])
        S.retire(QKV_SCR + P_SCR + S_SCR + G_SCR + L0_SCR + ATT + ["sqA", "sqB"], ["ostage"])
        mlp(1, do_norm=False, tile_done=store_tile)
        L1_OUT = [("o_nck", c) for c in range(4)] + [("o_ncv", c) for c in range(4)] + \
                 [("o_ndk", c) for c in range(4)] + [("o_ndv", c) for c in range(4)]


    if debug_stage is not None:
        S.retire(["pooled", "uBall", "uBp", "uBst", "Pp", "hid", "relu"], ["ostage"])
        for ti in range(3):
            store_tile(ti)
    S.add("sp", lambda e: e.nop(), r=[("outd", (0, c)) for c in range(4)] + [("outd", (512, c)) for c in range(8)] + L1_OUT, w=())

    S.emit(nc, es)
    es.close()
    return nc


_CACHE = {}


def kernel(**inputs):
    inp = {k: np.asarray(v) for k, v in inputs.items()}
    if "nc" not in _CACHE:
        _CACHE["nc"] = build_program(DEBUG_STAGE)
    nc = _CACHE["nc"]
    xpr, xsm = inp["x_prompt"], inp["x_sample"]
    in_maps = []
    for core in range(NCORES):
        b, q = core // 4, core % 4
        m = {}
        m["xp"] = np.ascontiguousarray(xpr[2 * core:2 * core + 2].reshape(512, D))
        m["xs"] = np.ascontiguousarray(xsm[b, 1024 * q:1024 * q + 1024])
        xh = np.zeros((16, D), np.float32)
        if q > 0:
            xh[0:8] = xsm[b, 1024 * q - 8:1024 * q]
        if q < 3:
            xh[8:16] = xsm[b, 1024 * q + 1024:1024 * q + 1032]
        m["xh"] = xh
        m["cond"] = np.ascontiguousarray(np.stack([inp["c_ctx"], inp["c"][b]], 0))
        m["norm1_g"] = inp["norm1_g"]
        m["norm2_g"] = inp["norm2_g"]
        m["w_ada"] = inp["w_ada"]
        m["b_ada"] = inp["b_ada"]
        m["w_in_even"] = inp["w_in_even"][0]
        m["w_pool"] = inp["w_pool"][0]
        m["pool_scale"] = inp["pool_scale"][0]
        m["w_fft"] = inp["w_fft"][0]
        m["w_out_even"] = inp["w_out_even"][0]
        m["w_mlp1"] = inp["w_mlp1"]
        m["w_mlp2"] = inp["w_mlp2"]
        if DEBUG_STAGE is None:
            m["w_in_odd"] = inp["w_in_odd"][0]
            m["w_out_odd"] = inp["w_out_odd"][0]
            for nm in ("c_qn_g", "c_kn_g", "d_qn_g", "d_kn_g", "c_subln_g", "d_sink"):
                m[nm] = np.ascontiguousarray(inp[nm][0])
            m["lam4"] = np.ascontiguousarray(np.stack([inp["lam_q1"][0], inp["lam_k1"][0], inp["lam_q2"][0], inp["lam_k2"][0]], 0))
            m["cache_c_k"] = np.ascontiguousarray(inp["cache_c_k"][b, 0].reshape(512, 512))
            m["cache_c_v"] = np.ascontiguousarray(inp["cache_c_v"][b, 0].reshape(512, 512))
            m["cache_d_k"] = np.ascontiguousarray(inp["cache_d_k"][b, 0].reshape(512, 128))
            m["cache_d_v"] = np.ascontiguousarray(inp["cache_d_v"][b, 0].reshape(512, 128))
        cst = host_constants(core)
        if DEBUG_STAGE is not None:
            for nm in ("rope", "band", "mg"):
                cst.pop(nm)
        m.update(cst)
        in_maps.append(m)
    res = run_bass_kernel_spmd(nc, in_maps, core_ids=list(range(NCORES)))
    R = res.results
    y_prompt = np.concatenate([R[c]["y_prompt"].reshape(2, 256, D) for c in range(NCORES)], 0)
    y_sample = np.stack([np.concatenate([R[4 * b + q]["y_sample"] for q in range(4)], 0) for b in range(2)], 0)
    if DEBUG_STAGE is not None:
        return y_prompt, y_sample
    nck = np.concatenate([R[c]["new_c_k"].reshape(2, 1, 256, 4, 2, 64) for c in range(NCORES)], 0)
    ncv = np.concatenate([R[c]["new_c_v"].reshape(2, 1, 256, 4, 128) for c in range(NCORES)], 0)
    ndk = np.concatenate([R[c]["new_d_k"].reshape(2, 1, 256, 2, 64) for c in range(NCORES)], 0)
    ndv = np.concatenate([R[c]["new_d_v"].reshape(2, 1, 256, 2, 64) for c in range(NCORES)], 0)
    return y_prompt, y_sample, nck, ncv, ndk, ndv
```
